# Optimizing a Trainium2 kernel written in Bass

```python
import jax, jax.numpy as jnp
from jax import lax
import numpy as np

D_MODEL = 4096
BATCH = 4
SEQ = 2048
DEPTH = 1

CHUNK = 64
N_LEFT_CHUNKS = 8
BAND = N_LEFT_CHUNKS + 1
A_WIDTH = D_MODEL // 2
A_HEAD_DIM = 128
A_HEADS = A_WIDTH // A_HEAD_DIM
REL_MAX = 128
REL_SIZE = REL_MAX + CHUNK
B_WIDTH = D_MODEL - A_WIDTH
POOL_WINDOWS = (2, 4, 8, 16)
B_GROUPS = len(POOL_WINDOWS)
B_GROUP_DIM = B_WIDTH // B_GROUPS
MIX_WIDTH = A_WIDTH + B_WIDTH
IN_COLS = 4 * A_WIDTH + 2 * B_WIDTH
XA_HEADS = 4
XA_HEAD_DIM = D_MODEL // 16
XA_WIDTH = XA_HEADS * XA_HEAD_DIM
N_MEM = 256
EPS = 1e-6
NEG_INF = -1e30

kernel_name = "hybrid_chunked_attn_multiscale_pool_block"


def rmsnorm(x, g):
    xf = x.astype(jnp.float32)
    y = xf * lax.rsqrt(jnp.mean(xf * xf, axis=-1, keepdims=True) + EPS)
    return (y * g.astype(jnp.float32)).astype(x.dtype)


def chunked_rel_attention(q, k, v, rel_bias):
    b, s, h, dh = q.shape
    nc = s // CHUNK
    qc = (q * (dh ** -0.5)).reshape(b, nc, CHUNK, h, dh)
    pad = ((0, 0), (N_LEFT_CHUNKS * CHUNK, 0), (0, 0), (0, 0))
    kp = jnp.pad(k, pad).reshape(b, nc + N_LEFT_CHUNKS, CHUNK, h, dh)
    vp = jnp.pad(v, pad).reshape(b, nc + N_LEFT_CHUNKS, CHUNK, h, dh)
    band_idx = jnp.arange(nc)[:, None] + jnp.arange(BAND)[None, :]
    kb = kp[:, band_idx].reshape(b, nc, BAND * CHUNK, h, dh)
    vb = vp[:, band_idx].reshape(b, nc, BAND * CHUNK, h, dh)
    scores = jnp.einsum('bcqhd,bckhd->bchqk', qc, kb).astype(jnp.float32)
    qi = jnp.arange(CHUNK)[:, None]
    kj = jnp.arange(BAND * CHUNK)[None, :]
    dist = qi + N_LEFT_CHUNKS * CHUNK - kj
    rel_idx = jnp.clip(dist, -(CHUNK - 1), REL_MAX) + (CHUNK - 1)
    bias = rel_bias[:, rel_idx].astype(jnp.float32)
    key_chunk = jnp.arange(nc)[:, None] + jnp.repeat(jnp.arange(BAND), CHUNK)[None, :] - N_LEFT_CHUNKS
    valid = (key_chunk >= 0)[None, :, None, None, :]
    scores = jnp.where(valid, scores + bias[None, None], NEG_INF)
    probs = jax.nn.softmax(scores, axis=-1).astype(v.dtype)
    out = jnp.einsum('bchqk,bckhd->bcqhd', probs, vb)
    return out.reshape(b, s, h * dh)


def multiscale_pool(u, w_pool, pool_scale):
    b, s, c = u.shape
    uf = u.astype(jnp.float32)
    cs = jnp.pad(jnp.cumsum(uf, axis=1), ((0, 0), (1, 0), (0, 0)))
    t = jnp.arange(s)
    groups = []
    for gi, w in enumerate(POOL_WINDOWS):
        lo, hi = gi * B_GROUP_DIM, (gi + 1) * B_GROUP_DIM
        start = jnp.maximum(t + 1 - w, 0)
        count = (t + 1 - start).astype(jnp.float32)
        mean = (cs[:, 1:, lo:hi] - cs[:, start, lo:hi]) / count[None, :, None]
        groups.append(mean - uf[:, :, lo:hi])
    d = jnp.stack(groups, axis=2).astype(u.dtype)
    y = jnp.einsum('bsgi,gio->bsgo', d, w_pool).reshape(b, s, c)
    return y * pool_scale


def mixer_sublayer(x, g_pre, w_in, rel_bias, w_pool, pool_scale, w_out, g_post):
    b, s, _ = x.shape
    h = rmsnorm(x, g_pre)
    proj = h @ w_in
    q, k, v, gate_a, u_b, gate_b = jnp.split(
        proj, [A_WIDTH, 2 * A_WIDTH, 3 * A_WIDTH, 4 * A_WIDTH, 4 * A_WIDTH + B_WIDTH], axis=-1)
    shp = (b, s, A_HEADS, A_HEAD_DIM)
    att = chunked_rel_attention(q.reshape(shp), k.reshape(shp), v.reshape(shp), rel_bias)
    y_a = att * jax.nn.silu(gate_a)
    y_b = multiscale_pool(u_b, w_pool, pool_scale) * jax.nn.silu(gate_b)
    y = jnp.concatenate([y_a, y_b], axis=-1) @ w_out
    return x + rmsnorm(y, g_post)


def memory_xattn_sublayer(x, m, g_pre, w_xq, w_xk, w_xv, w_xo, g_post):
    b, s, _ = x.shape
    h = rmsnorm(x, g_pre)
    q = (h @ w_xq).reshape(b, s, XA_HEADS, XA_HEAD_DIM) * (XA_HEAD_DIM ** -0.5)
    k = (m @ w_xk).reshape(b, m.shape[1], XA_HEADS, XA_HEAD_DIM)
    v = (m @ w_xv).reshape(b, m.shape[1], XA_HEADS, XA_HEAD_DIM)
    scores = jnp.einsum('bshd,bmhd->bhsm', q, k).astype(jnp.float32)
    probs = jax.nn.softmax(scores, axis=-1).astype(v.dtype)
    out = jnp.einsum('bhsm,bmhd->bshd', probs, v).reshape(b, s, XA_WIDTH)
    return x + rmsnorm(out @ w_xo, g_post)


def setup_inputs(seed: int = 0) -> dict:
    key = jax.random.key(seed)
    ks = jax.random.split(key, 18)
    f32 = jnp.float32

    def nrm(k, shape, scale):
        return jax.random.normal(k, shape, f32) * scale

    def gain(k, shape):
        return 1.0 + 0.05 * jax.random.normal(k, shape, f32)

    return {
        "x": nrm(ks[0], (BATCH, SEQ, D_MODEL), 1.0),
        "mem": nrm(ks[1], (BATCH, N_MEM, D_MODEL), 1.0),
        "g_mix_pre": gain(ks[2], (DEPTH, D_MODEL)),
        "w_in": nrm(ks[3], (DEPTH, D_MODEL, IN_COLS), D_MODEL ** -0.5),
        "rel_bias": nrm(ks[4], (DEPTH, A_HEADS, REL_SIZE), 0.5),
        "w_pool": nrm(ks[5], (DEPTH, B_GROUPS, B_GROUP_DIM, B_GROUP_DIM), B_GROUP_DIM ** -0.5),
        "pool_scale": gain(ks[6], (DEPTH, B_WIDTH)),
        "w_out": nrm(ks[7], (DEPTH, MIX_WIDTH, D_MODEL), MIX_WIDTH ** -0.5),
        "g_mix_post": gain(ks[8], (DEPTH, D_MODEL)),
        "g_xa_pre": gain(ks[9], (DEPTH, D_MODEL)),
        "g_mem": gain(ks[10], (DEPTH, D_MODEL)),
        "w_xq": nrm(ks[11], (DEPTH, D_MODEL, XA_WIDTH), D_MODEL ** -0.5),
        "w_xk": nrm(ks[12], (DEPTH, D_MODEL, XA_WIDTH), D_MODEL ** -0.5),
        "w_xv": nrm(ks[13], (DEPTH, D_MODEL, XA_WIDTH), D_MODEL ** -0.5),
        "w_xo": nrm(ks[14], (DEPTH, XA_WIDTH, D_MODEL), XA_WIDTH ** -0.5),
        "g_xa_post": gain(ks[15], (DEPTH, D_MODEL)),
    }


def reference(x, mem, g_mix_pre, w_in, rel_bias, w_pool, pool_scale, w_out, g_mix_post,
              g_xa_pre, g_mem, w_xq, w_xk, w_xv, w_xo, g_xa_post):
    for layer in range(DEPTH):
        x = mixer_sublayer(x, g_mix_pre[layer], w_in[layer], rel_bias[layer], w_pool[layer],
                           pool_scale[layer], w_out[layer], g_mix_post[layer])
        m = rmsnorm(mem, g_mem[layer])
        x = memory_xattn_sublayer(x, m, g_xa_pre[layer], w_xq[layer], w_xk[layer], w_xv[layer],
                                  w_xo[layer], g_xa_post[layer])
    return x
```

```python
import numpy as np
import concourse.bass as bass
import concourse.mybir as mybir
from concourse.bass_utils import run_bass_kernel_spmd

F32 = mybir.dt.float32
BF16 = mybir.dt.bfloat16
AF = mybir.ActivationFunctionType
ALU = mybir.AluOpType

D = 4096
T = 1024
HALO = 512
TT = T + HALO
NMEM = 256
KC = D // 128
EPS = 1e-6
NW = 3
NEG = -1e30


class Buf:
    __slots__ = ("w", "r")

    def __init__(self):
        self.w = None
        self.r = {}


class Stream:
    def __init__(self, name, is_dma_only=False):
        self.name = name
        self.ops = []
        self.cnt = 0
        self.waited = {}
        self.dma_i = 0
        self.dma_out = {}


class Prog:
    NDS = 8

    def __init__(self):
        self.streams = {n: Stream(n) for n in ("pe", "act", "dve", "pool", "sp")}
        self.sems = {}

    def sem_keys(self):
        keys = []
        for n in self.streams:
            keys.append("c_" + n)
        for n in ("pool", "sp", "act"):
            for i in range(self.NDS):
                keys.append("d_%s_%d" % (n, i))
        return keys

    def _waits(self, s, deps):
        waits = []
        for d in deps:
            if d is None:
                continue
            k, v = d
            if s.waited.get(k, 0) >= v:
                continue
            s.waited[k] = v
            waits.append((k, v))
        return waits

    def op(self, eng, fn, reads=(), writes=(), extra=()):
        s = self.streams[eng]
        own = "c_" + eng
        deps = list(extra)
        for b in reads:
            if b.w is not None:
                deps.append(b.w)
        for b in writes:
            if b.w is not None:
                deps.append(b.w)
            deps.extend(b.r.items())
        waits = self._waits(s, deps)
        s.cnt += 1
        tok = (own, s.cnt)
        s.ops.append((waits, fn, tok, 1))
        self._track(tok, reads, writes)
        return tok

    def dma(self, eng, fn, reads=(), writes=(), extra=()):
        s = self.streams[eng]
        nds = 3 if eng == "pool" else self.NDS
        slot = s.dma_i % nds
        rnd = s.dma_i // nds
        s.dma_i += 1
        key = "d_%s_%d" % (eng, slot)
        deps = list(extra)
        if rnd > 0:
            deps.append((key, 16 * rnd))
        for b in reads:
            if b.w is not None:
                deps.append(b.w)
        for b in writes:
            if b.w is not None:
                deps.append(b.w)
            deps.extend(b.r.items())
        waits = self._waits(s, deps)
        tok = (key, 16 * (rnd + 1))
        s.ops.append((waits, fn, tok, 16))
        s.dma_out[key] = tok[1]
        self._track(tok, reads, writes)
        return tok

    def _track(self, tok, reads, writes):
        for b in reads:
            if b.r.get(tok[0], 0) < tok[1]:
                b.r[tok[0]] = tok[1]
        for b in writes:
            b.w = tok
            b.r = {}

    def barrier(self):
        toks = []
        for n, s in self.streams.items():
            if s.cnt > 0:
                toks.append(("c_" + n, s.cnt))
            toks.extend(s.dma_out.items())
        for n, s in self.streams.items():
            waits = self._waits(s, toks)
            if waits:
                s.ops.append((waits, None, None, 0))

    def emit(self, eng, h):
        s = self.streams[eng]
        for waits, fn, tok, inc in s.ops:
            for k, v in waits:
                h.wait_ge(self.sems[k], v)
            if fn is None:
                continue
            ins = fn(h)
            ins.then_inc(self.sems[tok[0]], inc)


class Arena:
    def __init__(self, ap_bf16, nbytes):
        self.ap = ap_bf16
        self.nbytes = nbytes
        self.off = 0

    def alloc_at(self, off, shape, dtype):
        save = self.off
        self.off = off
        v = self.alloc(shape, dtype)
        self.off = save
        return v

    def alloc(self, shape, dtype):
        n = 1
        for d in shape:
            n *= d
        esz = 4 if dtype == F32 else 2
        nb = (n * esz + 63) // 64 * 64
        assert self.off + nb <= self.nbytes, ("SBUF arena overflow", self.off, nb, self.nbytes)
        v = self.ap[:, self.off // 2:(self.off + n * esz) // 2]
        self.off += nb
        if dtype == F32:
            v = v.bitcast(F32)
        if len(shape) == 2:
            v = v.rearrange("p (a b) -> p a b", a=shape[0])
        elif len(shape) == 3:
            v = v.rearrange("p (a b c) -> p a b c", a=shape[0], b=shape[1])
        return v


def build_program(debug=False):
    nc = bass.Bass("TRN2", target_bir_lowering=False)

    def din(name, shape, dt=F32):
        return nc.dram_tensor(name, list(shape), dt, kind="ExternalInput").ap()

    x_ext = din("x_ext", [TT, D])
    mem = din("mem", [NMEM, D])
    w_in = din("w_in", [D, 12288])
    w_pool = din("w_pool", [2048, 512])
    w_out = din("w_out", [D, D])
    w_xq = din("w_xq", [D, 1024])
    w_xk = din("w_xk", [D, 1024])
    w_xv = din("w_xv", [D, 1024])
    w_xo = din("w_xo", [1024, D])
    g_mix_pre = din("g_mix_pre", [1, D])
    g_mix_post = din("g_mix_post", [1, D])
    g_xa_pre = din("g_xa_pre", [1, D])
    g_mem = din("g_mem", [1, D])
    g_xa_post = din("g_xa_post", [1, D])
    pscale_in = din("pool_scale_col", [128, 16])
    bias_tab = din("bias_tab", [16, 128, 640])
    halo_in = din("halo_w", [128, 128])
    corr_in = din("corr", [128, 64])
    ident_in = din("ident", [128, 128])
    out = nc.dram_tensor("out", [T, D], F32, kind="ExternalOutput").ap()
    skind = "ExternalOutput" if debug else "Internal"
    ysT = nc.dram_tensor("ysT", [128, KC, T], BF16, kind=skind).ap()
    zs = nc.dram_tensor("zs", [T, D], F32, kind=skind).ap()
    x1s = nc.dram_tensor("x1s", [T, D], F32, kind=skind).ap()

    P = Prog()
    ARENA_BYTES = 207 * 1024

    with (
        nc.sbuf_tensor("arena", [128, ARENA_BYTES // 2], BF16) as arena_t,
        nc.psum_tensor("ps", [128, 4096], F32) as ps,
    ):
        A = Arena(arena_t[:, :], ARENA_BYTES)
        bank = [ps[:, b * 512:(b + 1) * 512] for b in range(8)]
        bankB = [Buf() for _ in range(8)]

        ident_f = A.alloc([128], F32)
        ident_b = A.alloc([128], BF16)
        ones_b = A.alloc([128], BF16)
        halo_f = A.alloc([128], F32)
        halo_b = A.alloc([128], BF16)
        pscale = A.alloc([16], F32)
        corr = A.alloc([64], F32)
        stat = A.alloc([64], F32)
        eps_col = A.alloc([2], F32)
        zstat = A.alloc([64], F32)
        zsum = A.alloc([8], F32)
        rstdz = A.alloc([8], F32)
        TOP = ARENA_BYTES - 8192
        K2T = A.alloc_at(TOP, [8, NMEM], BF16)
        K2TB = Buf()
        V2 = A.alloc_at(TOP + 4096, [2, 1024], BF16)
        V2B = Buf()
        wslot = [A.alloc([4096], BF16) for _ in range(NW)]
        wslotB = [Buf() for _ in range(NW)]
        constB = Buf()
        statBs = {}

        def sB(c):
            if c not in statBs:
                statBs[c] = Buf()
            return statBs[c]
        MARK0 = A.off

        units = []

        def wview(w, col0, ncol, kc0=0, kcn=None):
            v = w.rearrange("(kc p) n -> p kc n", p=128)
            kcn = v.shape[1] if kcn is None else kcn
            return v[:, kc0:kc0 + kcn, col0:col0 + ncol]

        for h in range(16):
            for base in (0, 2048, 4096, 6144):
                units.append((wview(w_in, base + h * 128, 128), 32, 128))
        for g in range(4):
            for c in range(4 * g, 4 * g + 4):
                units.append((wview(w_in, 8192 + c * 128, 128), 32, 128))
            for c in range(4 * g, 4 * g + 4):
                units.append((wview(w_in, 10240 + c * 128, 128), 32, 128))
        for u in range(8):
            units.append((wview(w_xk, u * 128, 128), 32, 128))
        for u in range(8):
            units.append((wview(w_xv, u * 128, 128), 32, 128))
        for u in range(8):
            units.append((wview(w_xq, u * 128, 128), 32, 128))
        wstate = {"loaded": 0, "next": 0}

        def w_load_upto(i):
            while wstate["loaded"] <= i and wstate["loaded"] < len(units):
                j = wstate["loaded"]
                src, kcn, ncol = units[j]
                dst = wslot[j % NW][:, 0:kcn * ncol].rearrange("p (k n) -> p k n", k=kcn)
                P.dma("pool", lambda h, d=dst, s=src: h.dma_start(out=d, in_=s),
                      writes=[wslotB[j % NW]])
                wstate["loaded"] += 1

        def w_get():
            i = wstate["next"]
            wstate["next"] += 1
            w_load_upto(i + NW - 1)
            src, kcn, ncol = units[i]
            v = wslot[i % NW][:, 0:kcn * ncol].rearrange("p (k n) -> p k n", k=kcn)
            return v, wslotB[i % NW]

        def issue_consts():
            cB = [Buf() for _ in range(4)]
            P.dma("sp", lambda h: h.dma_start(out=ident_f, in_=ident_in), writes=[cB[0]])
            P.dma("sp", lambda h: h.dma_start(out=halo_f, in_=halo_in), writes=[cB[1]])
            P.dma("sp", lambda h: h.dma_start(out=pscale, in_=pscale_in), writes=[cB[2]])
            P.dma("sp", lambda h: h.dma_start(out=corr, in_=corr_in), writes=[cB[3]])
            P.op("dve", lambda h: h.tensor_copy(out=ident_b, in_=ident_f), reads=[cB[0]], writes=[constB])
            P.op("dve", lambda h: h.tensor_copy(out=halo_b, in_=halo_f), reads=[cB[1]], writes=[constB])
            P.op("dve", lambda h: h.memset(ones_b, 2.0), writes=[constB])
            P.op("dve", lambda h: h.memset(eps_col, EPS), writes=[constB])
            P.op("dve", lambda h: h.tensor_scalar(out=pscale, in0=pscale, scalar1=0.5, scalar2=None,
                                                  op0=ALU.mult), reads=[cB[2], cB[3]], writes=[constB])
        w_load_upto(NW - 1)

        proj_rr = {"i": 0}

        def next_proj_bank():
            b = proj_rr["i"] % 2
            proj_rr["i"] += 1
            return b

        def norm_A(src, srcB, gbc, gbcB, hn, hnB, junk, junkB, scol):
            for c in range(2):
                P.op("act", lambda h, c=c: h.activation(out=junk, in_=src[:, c * 2048:(c + 1) * 2048],
                                                        func=AF.Square, accum_out=stat[:, scol + c:scol + c + 1]),
                     reads=[srcB], writes=[junkB, sB(scol + c)])
            P.op("dve", lambda h: h.tensor_tensor(out=stat[:, scol + 2:scol + 3], in0=stat[:, scol:scol + 1],
                                                  in1=stat[:, scol + 1:scol + 2], op=ALU.add),
                 reads=[sB(scol), sB(scol + 1)], writes=[sB(scol + 2)])
            P.op("act", lambda h: h.activation(out=stat[:, scol + 3:scol + 4], in_=stat[:, scol + 2:scol + 3],
                                               func=AF.Sqrt, scale=1.0 / D, bias=eps_col[:, 0:1]),
                 reads=[sB(scol + 2), constB], writes=[sB(scol + 3)])
            P.op("dve", lambda h: h.reciprocal(out=stat[:, scol + 4:scol + 5], in_=stat[:, scol + 3:scol + 4]),
                 reads=[sB(scol + 3)], writes=[sB(scol + 4)])
            P.op("dve", lambda h: h.scalar_tensor_tensor(out=hn, in0=src, scalar=stat[:, scol + 4:scol + 5],
                                                         in1=gbc, op0=ALU.mult, op1=ALU.mult),
                 reads=[srcB, sB(scol + 4), gbcB], writes=[hnB])

        def norm_B(hn, hnB, dst_fn, dstBs):
            for q in range(4):
                b = 4 + q
                pvb = bank[b].bitcast(BF16)

                def tr(h, q=q, pvb=pvb):
                    ins = None
                    for j in range(8):
                        kc = q * 8 + j
                        ins = h.transpose(out=pvb[:, j * 128:(j + 1) * 128],
                                          in_=hn[:, kc * 128:(kc + 1) * 128], identity=ident_b)
                    return ins
                P.op("pe", tr, reads=[hnB, constB], writes=[bankB[b]])
                dst = dst_fn(q)
                srcv = pvb.rearrange("p (k t) -> p k t", k=8)
                if q % 2 == 0:
                    P.op("act", lambda h, d=dst, s=srcv: h.activation(out=d, in_=s, func=AF.Copy),
                         reads=[bankB[b]], writes=[dstBs[0]])
                else:
                    P.op("dve", lambda h, d=dst, s=srcv: h.tensor_copy(out=d, in_=s),
                         reads=[bankB[b]], writes=[dstBs[1]])

        def norm_to_featmajor(src, srcB, gbc, gbcB, hn, hnB, junk, junkB, scol, dst_fn, dstBs):
            norm_A(src, srcB, gbc, gbcB, hn, hnB, junk, junkB, scol)
            norm_B(hn, hnB, dst_fn, dstBs)

        hT = A.alloc([KC, TT], BF16)
        hTB = [[Buf(), Buf()] for _ in range(TT // 128)]
        MARK_H = A.off
        xt0 = [A.alloc([D], F32) for _ in range(3)]
        xt0B = [Buf(), Buf(), Buf()]
        hn0 = [A.alloc([D], BF16) for _ in range(2)]
        hn0B = [Buf(), Buf()]
        junk0 = A.alloc([2048], BF16)
        junk0B = Buf()
        gbc0 = A.alloc([D], F32)
        gbc0B = Buf()
        NT0 = TT // 128

        def p0_load(i):
            P.dma("sp", lambda h: h.dma_start(out=xt0[i % 3], in_=x_ext[i * 128:(i + 1) * 128, :]),
                  writes=[xt0B[i % 3]])

        def p0_norm(i):
            norm_A(xt0[i % 3], xt0B[i % 3], gbc0, gbc0B, hn0[i % 2], hn0B[i % 2], junk0, junk0B, 5 * (i % 4))
        issue_consts()
        p0_load(0)
        P.dma("sp", lambda h: h.dma_start(out=gbc0, in_=g_mix_pre.partition_broadcast(128)[:, 0, :]),
              writes=[gbc0B])
        p0_load(1)
        p0_load(2)
        p0_norm(0)
        for i in range(NT0):
            if i + 1 < NT0:
                if i + 3 < NT0:
                    p0_load(i + 3)
                p0_norm(i + 1)
            norm_B(hn0[i % 2], hn0B[i % 2],
                   lambda q, i=i: hT[:, q * 8:q * 8 + 8, i * 128:(i + 1) * 128], hTB[i])
        P.barrier()
        A.off = MARK_H

        qT = [A.alloc([T], BF16) for _ in range(2)]
        kT = [A.alloc([TT], BF16) for _ in range(2)]
        vT = A.alloc([TT], BF16)
        V = [A.alloc([12, 128], BF16) for _ in range(2)]
        sg = [A.alloc([T], F32) for _ in range(2)]
        th = A.alloc([512], F32)
        btab = [A.alloc([640], F32) for _ in range(2)]
        sc = [A.alloc([640], F32) for _ in range(2)]
        pT = [A.alloc([640], BF16) for _ in range(2)]
        rc = [A.alloc([128], F32) for _ in range(2)]
        tt_ = [A.alloc([128], F32) for _ in range(2)]
        yst = [A.alloc([T], BF16) for _ in range(2)]
        qTB = [Buf(), Buf()]
        kTB = [Buf(), Buf()]
        vTB = Buf()
        VB = [Buf(), Buf()]
        sgB = [Buf(), Buf()]
        thB = Buf()
        btabB = [Buf(), Buf()]
        scB = [Buf(), Buf()]
        pTB = [Buf(), Buf()]
        rcB = [Buf(), Buf()]
        ttB = [Buf(), Buf()]
        ystB = [Buf(), Buf()]
        ysTB = [Buf() for _ in range(32)]

        def proj(wv, wB, kcn, rhs_fn, rhsB_fn, blocks, evac_fn):
            for bi, (t0, n) in enumerate(blocks):
                b = next_proj_bank()

                def mm(h, t0=t0, n=n, b=b):
                    ins = None
                    for kc in range(kcn):
                        ins = h.matmul(bank[b][:, 0:n], lhsT=wv[:, kc, :], rhs=rhs_fn(kc, t0, n),
                                       start=(kc == 0), stop=(kc == kcn - 1))
                    return ins
                P.op("pe", mm, reads=[wB] + list(rhsB_fn(t0, n)), writes=[bankB[b]])
                evac_fn(bi, t0, n, bank[b][:, 0:n], bankB[b])

        def hT_rhs(kc, t0, n):
            return hT[:, kc, t0:t0 + n]

        def hT_bufs(t0, n):
            r = []
            for i in range(t0 // 128, (t0 + n + 127) // 128):
                r.extend(hTB[i])
            return r

        OWN = [(HALO, 512), (HALO + 512, 512)]
        ALL = [(0, 512), (512, 512), (1024, 512)]
        SCALE_A = 128 ** -0.5

        def attn_chunks(hd):
            hb = hd % 2

            def qk(qb):
                sb = qb % 2
                bA, bB2 = (3, 4) if sb == 0 else (5, 6)

                def f(h):
                    ins = None
                    for j in range(5):
                        o = bank[bA][:, j * 128:(j + 1) * 128] if j < 4 else bank[bB2][:, 0:128]
                        ins = h.matmul(o, lhsT=kT[hb][:, (qb + j) * 128:(qb + j + 1) * 128],
                                       rhs=qT[hb][:, qb * 128:(qb + 1) * 128], start=True, stop=True)
                    return ins
                P.op("pe", f, reads=[kTB[hb], qTB[hb]], writes=[bankB[bA], bankB[bB2]])
                sview = ps[:, bA * 512:bA * 512 + 640]
                P.op("dve", lambda h: h.tensor_tensor(out=sc[sb], in0=sview, in1=btab[hb], op=ALU.add),
                     reads=[bankB[bA], bankB[bB2], btabB[hb]], writes=[scB[sb]])
                P.op("act", lambda h: h.activation(out=pT[sb], in_=sc[sb], func=AF.Exp),
                     reads=[scB[sb]], writes=[pTB[sb]])

            def pvden(qb):
                sb = qb % 2
                ob = 7 if qb % 2 == 0 else 2

                def f(h):
                    ins = None
                    for j in range(5):
                        ins = h.matmul(bank[ob][:, 0:128], lhsT=V[hb][:, qb + j, :],
                                       rhs=pT[sb][:, j * 128:(j + 1) * 128], start=(j == 0), stop=(j == 4))
                    for j in range(5):
                        lw = halo_b if (qb + j) < 4 else ones_b
                        ins = h.matmul(bank[ob][:, 128:256], lhsT=lw,
                                       rhs=pT[sb][:, j * 128:(j + 1) * 128], start=(j == 0), stop=(j == 4))
                    return ins
                P.op("pe", f, reads=[VB[hb], pTB[sb], constB], writes=[bankB[ob]])
                P.op("dve", lambda h: h.reciprocal(out=rc[sb], in_=bank[ob][:, 128:256]),
                     reads=[bankB[ob]], writes=[rcB[sb]])
                P.op("dve", lambda h: h.tensor_tensor(out=tt_[sb], in0=bank[ob][:, 0:128], in1=rc[sb], op=ALU.mult),
                     reads=[bankB[ob], rcB[sb]], writes=[ttB[sb]])
                P.op("dve", lambda h: h.tensor_tensor(out=yst[hb][:, qb * 128:(qb + 1) * 128], in0=tt_[sb],
                                                      in1=sg[hb][:, qb * 128:(qb + 1) * 128], op=ALU.mult),
                     reads=[ttB[sb], sgB[hb]], writes=[ystB[hb]])

            def store():
                P.dma("sp", lambda h: h.dma_start(out=ysT[:, hd, :], in_=yst[hb]),
                      reads=[ystB[hb]], writes=[ysTB[hd]])

            return [
                lambda: (qk(0), qk(1)),
                lambda: (pvden(0), qk(2), pvden(1), qk(3)),
                lambda: (pvden(2), qk(4), pvden(3), qk(5)),
                lambda: (pvden(4), qk(6), pvden(5), qk(7)),
                lambda: (pvden(6), pvden(7), store()),
            ]

        chunks = {}
        for it in range(18):
            hd = it
            hb = hd % 2
            if 1 <= it <= 16:
                chunks[it - 1] = attn_chunks(it - 1)
            if hd >= 16:
                if it == 16:
                    chunks[14][4]()
                    for c in chunks[15]:
                        c()
                break
            P.dma("sp", lambda h, hd=hd: h.dma_start(out=btab[hd % 2], in_=bias_tab[hd]),
                  writes=[btabB[hd % 2]])
            wv, wB = w_get()

            def ev_q(bi, t0, n, pa, pB, hb=hb):
                P.op("act", lambda h: h.activation(out=qT[hb][:, t0 - HALO:t0 - HALO + n], in_=pa,
                                                   func=AF.Copy, scale=SCALE_A),
                     reads=[pB], writes=[qTB[hb]])
            proj(wv, wB, 32, hT_rhs, hT_bufs, OWN, ev_q)
            if it >= 2:
                chunks[it - 2][4]()
            if it >= 1:
                chunks[it - 1][0]()
            wv, wB = w_get()

            def ev_k(bi, t0, n, pa, pB, hb=hb):
                P.op("dve", lambda h: h.tensor_copy(out=kT[hb][:, t0:t0 + n], in_=pa),
                     reads=[pB], writes=[kTB[hb]])
            proj(wv, wB, 32, hT_rhs, hT_bufs, ALL, ev_k)
            if it >= 1:
                chunks[it - 1][1]()
            wv, wB = w_get()

            def ev_v(bi, t0, n, pa, pB):
                P.op("act", lambda h: h.activation(out=vT[:, t0:t0 + n], in_=pa, func=AF.Copy),
                     reads=[pB], writes=[vTB])
            proj(wv, wB, 32, hT_rhs, hT_bufs, ALL, ev_v)
            for half in range(2):
                pv = bank[2].bitcast(BF16)

                def trv(h, half=half, pv=pv):
                    ins = None
                    for j in range(6):
                        t = half * 6 + j
                        ins = h.transpose(out=pv[:, j * 128:(j + 1) * 128],
                                          in_=vT[:, t * 128:(t + 1) * 128], identity=ident_b)
                    return ins
                P.op("pe", trv, reads=[vTB, constB], writes=[bankB[2]])
                P.op("dve", lambda h, half=half, pv=pv, hb=hb: h.tensor_copy(
                    out=V[hb][:, half * 6:half * 6 + 6, :],
                    in_=pv[:, 0:768].rearrange("p (a b) -> p a b", a=6)),
                    reads=[bankB[2]], writes=[VB[hb]])
            if it >= 1:
                chunks[it - 1][2]()
            wv, wB = w_get()

            def ev_g(bi, t0, n, pa, pB, hb=hb):
                P.op("act", lambda h: h.activation(out=th[:, 0:n], in_=pa, func=AF.Tanh, scale=0.5),
                     reads=[pB], writes=[thB])
                P.op("dve", lambda h: h.scalar_tensor_tensor(
                    out=sg[hb][:, t0 - HALO:t0 - HALO + n], in0=th[:, 0:n], scalar=1.0, in1=pa,
                    op0=ALU.add, op1=ALU.mult), reads=[pB, thB], writes=[sgB[hb]])
            proj(wv, wB, 32, hT_rhs, hT_bufs, OWN, ev_g)
            if it >= 1:
                chunks[it - 1][3]()

        P.barrier()
        A.off = MARK_H

        UW = T + 16
        u_ext = [A.alloc([UW], F32) for _ in range(2)]
        s_a = A.alloc([UW], F32)
        s_b = A.alloc([UW], F32)
        t16 = A.alloc([16], F32)
        dT = A.alloc([4, T], BF16)
        sgb = [A.alloc([T], F32) for _ in range(2)]
        thb = A.alloc([512], F32)
        ystb = [A.alloc([T], BF16) for _ in range(2)]
        wp = [A.alloc([4, 512], BF16) for _ in range(2)]
        wpB = [Buf(), Buf()]
        u_extB = [Buf(), Buf()]
        s_aB = Buf()
        s_bB = Buf()
        t16B = Buf()
        dTB = [Buf() for _ in range(4)]
        sgbB = [Buf(), Buf()]
        thbB = Buf()
        ystbB = [Buf(), Buf()]
        UBLK = [(HALO - 16, 512), (HALO - 16 + 512, 512), (HALO - 16 + 1024, 16)]

        for g in range(4):
            wlen = 2 << g
            P.dma("pool", lambda h, g=g: h.dma_start(out=wp[g % 2], in_=wview(w_pool, 0, 512, kc0=4 * g, kcn=4)),
                  writes=[wpB[g % 2]])
            for ci in range(4):
                c = 4 * g + ci
                ub = c % 2
                wv, wB = w_get()

                def ev_u(bi, t0, n, pa, pB, ub=ub):
                    o = t0 - (HALO - 16)
                    P.op("act", lambda h: h.activation(out=u_ext[ub][:, o:o + n], in_=pa, func=AF.Copy),
                         reads=[pB], writes=[u_extB[ub]])
                proj(wv, wB, 32, hT_rhs, hT_bufs, UBLK, ev_u)
                u = u_ext[ub]
                cur, curB, lo = u, u_extB[ub], 0
                tmp = [(s_a, s_aB), (s_b, s_bB)]
                step = 1
                k = 0
                while step < wlen:
                    dst, dstB = tmp[k % 2]
                    nlo = lo + step
                    P.op("dve", lambda h, dst=dst, cur=cur, nlo=nlo, step=step: h.tensor_tensor(
                        out=dst[:, nlo:UW], in0=cur[:, nlo:UW], in1=cur[:, nlo - step:UW - step], op=ALU.add),
                        reads=[curB], writes=[dstB])
                    cur, curB, lo = dst, dstB, nlo
                    step *= 2
                    k += 1
                P.op("dve", lambda h, cur=cur, u=u, ci=ci, wlen=wlen: h.scalar_tensor_tensor(
                    out=dT[:, ci, 16:T], in0=cur[:, 32:UW], scalar=1.0 / wlen, in1=u[:, 32:UW],
                    op0=ALU.mult, op1=ALU.subtract), reads=[curB, u_extB[ub]], writes=[dTB[ci]])
                P.op("dve", lambda h, cur=cur, g=g: h.tensor_tensor(
                    out=t16, in0=cur[:, 16:32], in1=corr[:, g * 16:(g + 1) * 16], op=ALU.mult),
                    reads=[curB, constB], writes=[t16B])
                P.op("dve", lambda h, u=u, ci=ci: h.tensor_tensor(
                    out=dT[:, ci, 0:16], in0=t16, in1=u[:, 16:32], op=ALU.subtract),
                    reads=[t16B, u_extB[ub]], writes=[dTB[ci]])
            wpv, wpB_ = wp[g % 2], wpB[g % 2]
            for oc in range(4):
                c = 4 * g + oc
                sb_ = c % 2
                wv, wB = w_get()

                def ev_gb(bi, t0, n, pa, pB, sb_=sb_):
                    P.op("act", lambda h: h.activation(out=thb[:, 0:n], in_=pa, func=AF.Tanh, scale=0.5),
                         reads=[pB], writes=[thbB])
                    P.op("dve", lambda h: h.scalar_tensor_tensor(
                        out=sgb[sb_][:, t0 - HALO:t0 - HALO + n], in0=thb[:, 0:n], scalar=1.0, in1=pa,
                        op0=ALU.add, op1=ALU.mult), reads=[pB, thbB], writes=[sgbB[sb_]])
                proj(wv, wB, 32, hT_rhs, hT_bufs, OWN, ev_gb)
                for blk in range(2):
                    b = next_proj_bank()

                    def mmp(h, blk=blk, b=b, oc=oc, wpv=wpv):
                        ins = None
                        for ic in range(4):
                            ins = h.matmul(bank[b], lhsT=wpv[:, ic, oc * 128:(oc + 1) * 128],
                                           rhs=dT[:, ic, blk * 512:(blk + 1) * 512],
                                           start=(ic == 0), stop=(ic == 3))
                        return ins
                    P.op("pe", mmp, reads=[wpB_] + dTB, writes=[bankB[b]])
                    P.op("dve", lambda h, blk=blk, b=b, c=c, sb_=sb_: h.scalar_tensor_tensor(
                        out=ystb[sb_][:, blk * 512:(blk + 1) * 512], in0=bank[b], scalar=pscale[:, c:c + 1],
                        in1=sgb[sb_][:, blk * 512:(blk + 1) * 512], op0=ALU.mult, op1=ALU.mult),
                        reads=[bankB[b], sgbB[sb_], constB], writes=[ystbB[sb_]])
                P.dma("sp", lambda h, c=c, sb_=sb_: h.dma_start(
                    out=ysT[:, 16 + c, :], in_=ystb[sb_]),
                    reads=[ystbB[sb_]], writes=[ysTB[16 + c]])

        P.barrier()
        A.off = MARK0

        mT = A.alloc([KC, NMEM], BF16)
        mTB = [[Buf(), Buf()], [Buf(), Buf()]]
        yT = A.alloc([KC, T], BF16)
        yTB = [Buf() for _ in range(4)]
        wbig = [None, None]
        wbig[0] = A.alloc([KC, 512], BF16)
        off_w1 = A.off
        wbig[1] = A.alloc([KC, 512], BF16)
        wbigB = [Buf(), Buf()]
        zst_b = [A.alloc([512], F32) for _ in range(2)]
        zst_bB = [Buf(), Buf()]
        junkz = A.alloc([512], BF16)
        junkzB = Buf()
        off_g = A.off
        gbcp = A.alloc([D], F32)
        gbcpB = Buf()
        v2T = A.alloc([NMEM], BF16)
        v2TB = Buf()
        off_end = A.off
        assert A.off <= TOP
        A.off = off_w1
        xtm = A.alloc([D], F32)
        gbcm = A.alloc([D], F32)
        A.off = off_g
        hnm = A.alloc([D], BF16)
        junkm = A.alloc([2048], BF16)
        A.off = off_end
        xtmB = Buf()
        gbcmB = Buf()
        hnmB = Buf()
        junkmB = Buf()

        zsB = [Buf() for _ in range(8)]
        zstatB = Buf()
        ysv = ysT
        for q in range(4):
            P.dma("sp", lambda h, q=q: h.dma_start(out=yT[:, q * 8:(q + 1) * 8, :], in_=ysv[:, q * 8:(q + 1) * 8, :]),
                  reads=ysTB[q * 8:(q + 1) * 8], writes=[yTB[q]])
        P.dma("sp", lambda h: h.dma_start(out=gbcm, in_=g_mem.partition_broadcast(128)[:, 0, :]), writes=[gbcmB])
        for i in range(2):
            P.dma("sp", lambda h, i=i: h.dma_start(out=xtm, in_=mem[i * 128:(i + 1) * 128, :]),
                  writes=[xtmB])
            norm_to_featmajor(xtm, xtmB, gbcm, gbcmB, hnm, hnmB, junkm, junkmB, 5 * i,
                              lambda q, i=i: mT[:, q * 8:q * 8 + 8, i * 128:(i + 1) * 128], mTB[i])
        P.dma("sp", lambda h: h.dma_start(out=gbcp, in_=g_mix_post.partition_broadcast(128)[:, 0, :]),
              writes=[gbcpB, hnmB, junkmB])
        MBLK = [(0, NMEM)]
        mT_all = mTB[0] + mTB[1]

        def mT_rhs(kc, t0, n):
            return mT[:, kc, t0:t0 + n]

        def kv_unit(j):
            wv, wB = w_get()
            if j < 8:
                def ev_k2(bi, t0, n, pa, pB, u=j):
                    P.op("act", lambda h: h.activation(out=K2T[:, u, :], in_=pa, func=AF.Copy),
                         reads=[pB], writes=[K2TB])
                proj(wv, wB, 32, mT_rhs, lambda t0, n: mT_all, MBLK, ev_k2)
            else:
                u = j - 8

                def ev_v2(bi, t0, n, pa, pB):
                    P.op("act", lambda h: h.activation(out=v2T, in_=pa, func=AF.Copy),
                         reads=[pB], writes=[v2TB])
                proj(wv, wB, 32, mT_rhs, lambda t0, n: mT_all, MBLK, ev_v2)
                pv = bank[2].bitcast(BF16)

                def trv2(h, pv=pv):
                    ins = None
                    for jj in range(2):
                        ins = h.transpose(out=pv[:, jj * 128:(jj + 1) * 128], in_=v2T[:, jj * 128:(jj + 1) * 128],
                                          identity=ident_b)
                    return ins
                P.op("pe", trv2, reads=[v2TB, constB], writes=[bankB[2]])
                P.op("dve", lambda h, u=u, pv=pv: h.tensor_copy(
                    out=V2[:, :, u * 128:(u + 1) * 128], in_=pv[:, 0:256].rearrange("p (a b) -> p a b", a=2)),
                    reads=[bankB[2]], writes=[V2B])

        def stats_to_rstd(ncol, src_cols, dst_col, stB, rsB):
            P.op("dve", lambda h: h.tensor_reduce(out=zsum[:, dst_col:dst_col + 1],
                                                  in_=zstat[:, src_cols:src_cols + ncol],
                                                  axis=mybir.AxisListType.X, op=ALU.add),
                 reads=[stB], writes=[rsB])
            P.op("act", lambda h: h.activation(out=zsum[:, dst_col:dst_col + 1], in_=zsum[:, dst_col:dst_col + 1],
                                               func=AF.Sqrt, scale=1.0 / D, bias=eps_col[:, 0:1]),
                 reads=[rsB, constB], writes=[rsB])
            P.op("dve", lambda h: h.reciprocal(out=rstdz[:, dst_col:dst_col + 1], in_=zsum[:, dst_col:dst_col + 1]),
                 reads=[rsB], writes=[rsB])

        zstB = [Buf() for _ in range(8)]
        rsB1 = [Buf() for _ in range(8)]

        def load_big(cb):
            extra_w = [xtmB, gbcmB] if cb == 1 else []
            P.dma("pool", lambda h, cb=cb: h.dma_start(out=wbig[cb % 2], in_=wview(w_out, cb * 512, 512)),
                  writes=[wbigB[cb % 2]] + extra_w)
        load_big(0)
        load_big(1)

        step = 0
        for cb in range(8):
            wv, wB = wbig[cb % 2], wbigB[cb % 2]
            for tile in range(8):
                b = next_proj_bank()

                def mm(h, tile=tile, b=b, wv=wv):
                    ins = None
                    for kc in range(32):
                        ins = h.matmul(bank[b], lhsT=yT[:, kc, tile * 128:(tile + 1) * 128], rhs=wv[:, kc, :],
                                       start=(kc == 0), stop=(kc == 31))
                    return ins
                P.op("pe", mm, reads=[wB] + yTB, writes=[bankB[b]])
                col = tile * 8 + cb
                ta = P.op("act", lambda h, b=b, col=col: h.activation(
                    out=junkz, in_=bank[b], func=AF.Square, accum_out=zstat[:, col:col + 1]),
                    reads=[bankB[b]], writes=[junkzB, zstB[tile]])
                zb = step % 2
                P.op("dve", lambda h, b=b, zb=zb, cb=cb: h.tensor_tensor(
                    out=zst_b[zb], in0=bank[b], in1=gbcp[:, cb * 512:(cb + 1) * 512], op=ALU.mult),
                    reads=[bankB[b], gbcpB], writes=[zst_bB[zb]], extra=[ta])
                P.dma("sp", lambda h, tile=tile, cb=cb, zb=zb: h.dma_start(
                    out=zs[tile * 128:(tile + 1) * 128, cb * 512:(cb + 1) * 512], in_=zst_b[zb]),
                    reads=[zst_bB[zb]], writes=[zsB[tile]])
                step += 1
                if step % 4 == 0:
                    kv_unit(step // 4 - 1)
            if cb + 2 < 8:
                load_big(cb + 2)
        P.op("dve", lambda h: h.tensor_reduce(out=zsum, in_=zstat.rearrange("p (a b) -> p a b", a=8),
                                              axis=mybir.AxisListType.X, op=ALU.add),
             reads=zstB, writes=rsB1)
        P.op("act", lambda h: h.activation(out=zsum, in_=zsum, func=AF.Sqrt, scale=1.0 / D, bias=eps_col[:, 0:1]),
             reads=rsB1 + [constB], writes=rsB1)
        P.op("dve", lambda h: h.reciprocal(out=rstdz, in_=zsum), reads=rsB1, writes=rsB1)
        P.barrier()
        A.off = MARK0

        h2T = A.alloc([KC, T], BF16)
        h2TB = [[Buf(), Buf()] for _ in range(8)]
        MARK_C = A.off
        zin1 = [A.alloc([D], F32) for _ in range(2)]
        zin1B = [Buf(), Buf()]
        xt1 = [A.alloc([D], F32) for _ in range(2)]
        xt1B = [Buf(), Buf()]
        hn1 = [A.alloc([D], BF16) for _ in range(2)]
        hn1B = [Buf(), Buf()]
        junk1 = A.alloc([2048], BF16)
        junk1B = Buf()
        gbc2 = A.alloc([D], F32)
        gbc2B = Buf()
        assert A.off <= TOP
        x1sB = [Buf() for _ in range(8)]
        P.dma("sp", lambda h: h.dma_start(out=gbc2, in_=g_xa_pre.partition_broadcast(128)[:, 0, :]),
              writes=[gbc2B])

        def pb2_A(i):
            b2 = i % 2
            P.dma("sp", lambda h: h.dma_start(out=zin1[b2], in_=zs[i * 128:(i + 1) * 128, :]),
                  reads=[zsB[i]], writes=[zin1B[b2]])
            P.dma("sp", lambda h: h.dma_start(out=xt1[b2], in_=x_ext[HALO + i * 128:HALO + (i + 1) * 128, :]),
                  writes=[xt1B[b2]])
            P.op("dve", lambda h: h.scalar_tensor_tensor(out=xt1[b2], in0=zin1[b2], scalar=rstdz[:, i:i + 1],
                                                         in1=xt1[b2], op0=ALU.mult, op1=ALU.add),
                 reads=[zin1B[b2], xt1B[b2], rsB1[i]], writes=[xt1B[b2]])
            P.dma("sp", lambda h: h.dma_start(out=x1s[i * 128:(i + 1) * 128, :], in_=xt1[b2]),
                  reads=[xt1B[b2]], writes=[x1sB[i]])
            norm_A(xt1[b2], xt1B[b2], gbc2, gbc2B, hn1[b2], hn1B[b2], junk1, junk1B, 20 + 5 * (i % 4))
        pb2_A(0)
        for i in range(8):
            if i + 1 < 8:
                pb2_A(i + 1)
            norm_B(hn1[i % 2], hn1B[i % 2],
                   lambda q, i=i: h2T[:, q * 8:q * 8 + 8, i * 128:(i + 1) * 128], h2TB[i])
        P.barrier()
        A.off = MARK_C

        o2T = A.alloc([8, T], BF16)
        o2TB = Buf()
        wxo = A.alloc([8, D], BF16)
        wxoB = [Buf() for _ in range(8)]
        MARK_D = A.off
        q2T = A.alloc([8, T], BF16)
        q2TB = Buf()
        p2T = [A.alloc([512], BF16) for _ in range(2)]
        p2TB = [Buf(), Buf()]
        rc2 = A.alloc([512], F32)
        rc2B = Buf()
        assert A.off <= TOP
        SCALE_X = 256 ** -0.5
        YBLK = [(0, 512), (512, 512)]

        def h2T_rhs(kc, t0, n):
            return h2T[:, kc, t0:t0 + n]

        def h2T_bufs(t0, n):
            r = []
            for i in range(t0 // 128, (t0 + n) // 128):
                r.extend(h2TB[i])
            return r

        for u in range(8):
            wv, wB = w_get()

            def ev_q2(bi, t0, n, pa, pB, u=u):
                P.op("act", lambda h: h.activation(out=q2T[:, u, t0:t0 + n], in_=pa, func=AF.Copy, scale=SCALE_X),
                     reads=[pB], writes=[q2TB])
            proj(wv, wB, 32, h2T_rhs, h2T_bufs, YBLK, ev_q2)
            P.dma("pool", lambda h, u=u: h.dma_start(out=wxo[:, :, u * 512:(u + 1) * 512],
                                                     in_=wview(w_xo, u * 512, 512)), writes=[wxoB[u]])

        for hd in range(4):
            for tb in range(2):
                t0 = tb * 512
                for mt in range(2):
                    b = 3 + mt

                    def s2(h, hd=hd, t0=t0, mt=mt, b=b):
                        ins = None
                        for dc in range(2):
                            ins = h.matmul(bank[b], lhsT=K2T[:, 2 * hd + dc, mt * 128:(mt + 1) * 128],
                                           rhs=q2T[:, 2 * hd + dc, t0:t0 + 512], start=(dc == 0), stop=(dc == 1))
                        return ins
                    P.op("pe", s2, reads=[K2TB, q2TB], writes=[bankB[b]])
                    P.op("act", lambda h, mt=mt, b=b: h.activation(out=p2T[mt], in_=bank[b], func=AF.Exp),
                         reads=[bankB[b]], writes=[p2TB[mt]])

                def dn(h):
                    ins = None
                    for mt in range(2):
                        ins = h.matmul(bank[7], lhsT=ones_b, rhs=p2T[mt], start=(mt == 0), stop=(mt == 1))
                    return ins
                P.op("pe", dn, reads=p2TB + [constB], writes=[bankB[7]])
                P.op("act", lambda h: h.activation(out=rc2, in_=bank[7], func=AF.Ln), reads=[bankB[7]], writes=[rc2B])
                P.op("act", lambda h: h.activation(out=rc2, in_=rc2, func=AF.Exp, scale=-1.0),
                     reads=[rc2B], writes=[rc2B])
                for dc in range(2):
                    b = 5 + dc

                    def o2(h, hd=hd, dc=dc, b=b):
                        ins = None
                        for mt in range(2):
                            ins = h.matmul(bank[b], lhsT=V2[:, mt, (2 * hd + dc) * 128:(2 * hd + dc + 1) * 128],
                                           rhs=p2T[mt], start=(mt == 0), stop=(mt == 1))
                        return ins
                    P.op("pe", o2, reads=p2TB + [V2B], writes=[bankB[b]])
                    P.op("dve", lambda h, hd=hd, dc=dc, b=b, t0=t0: h.scalar_tensor_tensor(
                        out=o2T[:, 2 * hd + dc, t0:t0 + 512], in0=bank[b], scalar=2.0, in1=rc2,
                        op0=ALU.mult, op1=ALU.mult), reads=[bankB[b], rc2B], writes=[o2TB])
        P.barrier()

        A.off = MARK0
        z2r = [A.alloc([D], F32) for _ in range(2)]
        z2rB = [Buf(), Buf()]
        xtf = [A.alloc([D], F32) for _ in range(2)]
        xtfB = [Buf(), Buf()]
        assert A.off <= MARK_C
        A.off = MARK_D
        gbcf = A.alloc([D], F32)
        gbcfB = Buf()
        junkc = A.alloc([512], BF16)
        junkcB = Buf()
        outB = [Buf() for _ in range(8)]
        P.dma("sp", lambda h: h.dma_start(out=gbcf, in_=g_xa_post.partition_broadcast(128)[:, 0, :]),
              writes=[gbcfB])
        zstB2 = [Buf(), Buf()]
        rsB2 = [Buf(), Buf()]

        def pc2_tail(tile):
            b2 = tile % 2
            stats_to_rstd(8, b2 * 8, b2, zstB2[b2], rsB2[b2])
            P.op("dve", lambda h: h.scalar_tensor_tensor(out=xtf[b2], in0=z2r[b2], scalar=rstdz[:, b2:b2 + 1],
                                                         in1=xtf[b2], op0=ALU.mult, op1=ALU.add),
                 reads=[z2rB[b2], xtfB[b2], rsB2[b2]], writes=[xtfB[b2]])
            P.dma("sp", lambda h: h.dma_start(out=out[tile * 128:(tile + 1) * 128, :], in_=xtf[b2]),
                  reads=[xtfB[b2]], writes=[outB[tile]])

        for tile in range(8):
            b2 = tile % 2
            P.dma("sp", lambda h, tile=tile, b2=b2: h.dma_start(out=xtf[b2], in_=x1s[tile * 128:(tile + 1) * 128, :]),
                  reads=[x1sB[tile]], writes=[xtfB[b2]])
            for cb in range(8):
                b = (tile * 8 + cb) % 6

                def mm2(h, tile=tile, cb=cb, b=b):
                    ins = None
                    for kc in range(8):
                        ins = h.matmul(bank[b], lhsT=o2T[:, kc, tile * 128:(tile + 1) * 128],
                                       rhs=wxo[:, kc, cb * 512:(cb + 1) * 512], start=(kc == 0), stop=(kc == 7))
                    return ins
                P.op("pe", mm2, reads=[o2TB, wxoB[cb]], writes=[bankB[b]])
                col = b2 * 8 + cb
                ta = P.op("act", lambda h, b=b, col=col: h.activation(
                    out=junkc, in_=bank[b], func=AF.Square, accum_out=zstat[:, col:col + 1]),
                    reads=[bankB[b]], writes=[junkcB, zstB2[b2]])
                P.op("dve", lambda h, b=b, b2=b2, cb=cb: h.tensor_tensor(
                    out=z2r[b2][:, cb * 512:(cb + 1) * 512], in0=bank[b], in1=gbcf[:, cb * 512:(cb + 1) * 512],
                    op=ALU.mult), reads=[bankB[b], gbcfB], writes=[z2rB[b2]], extra=[ta])
                if cb == 3 and tile >= 1:
                    pc2_tail(tile - 1)
        pc2_tail(7)
        P.barrier()

        keys = P.sem_keys()
        import contextlib
        with contextlib.ExitStack() as es:
            for k in keys:
                P.sems[k] = es.enter_context(nc.semaphore(k))
            with nc.Block() as block:
                @block.tensor
                def _(e):
                    P.emit("pe", e)

                @block.scalar
                def _(e):
                    P.emit("act", e)

                @block.vector
                def _(e):
                    P.emit("dve", e)

                @block.gpsimd
                def _(e):
                    P.emit("pool", e)

                @block.sync
                def _(e):
                    P.emit("sp", e)
    return nc


def _bias_table(rel_bias):
    kk = np.arange(128)[:, None, None]
    j = np.arange(5)[None, :, None]
    qq = np.arange(128)[None, None, :]
    koff = (j - 4) * 128 + kk
    dist = qq - koff
    idx = np.clip(dist, -63, 128) + 63
    cq = qq // 64
    ck = np.floor_divide(koff, 64)
    valid = (ck >= cq - 8) & (ck <= cq)
    tab = rel_bias[:, idx]
    tab = np.where(valid[None], tab, np.float32(NEG)).astype(np.float32)
    return np.ascontiguousarray(tab.reshape(16, 128, 640))


def make_in_maps(x, mem, g_mix_pre, w_in, rel_bias, w_pool, pool_scale, w_out, g_mix_post,
                 g_xa_pre, g_mem, w_xq, w_xk, w_xv, w_xo, g_xa_post):
    f = lambda a: np.ascontiguousarray(np.asarray(a, dtype=np.float32))
    x = f(x)
    mem = f(mem)
    shared = {
        "w_in": f(w_in[0]), "w_pool": f(np.asarray(w_pool[0]).reshape(2048, 512)), "w_out": f(w_out[0]),
        "w_xq": f(w_xq[0]), "w_xk": f(w_xk[0]), "w_xv": f(w_xv[0]), "w_xo": f(w_xo[0]),
        "g_mix_pre": f(g_mix_pre[0:1]), "g_mix_post": f(g_mix_post[0:1]), "g_xa_pre": f(g_xa_pre[0:1]),
        "g_mem": f(g_mem[0:1]), "g_xa_post": f(g_xa_post[0:1]),
        "pool_scale_col": f(np.asarray(pool_scale[0]).reshape(16, 128).T),
        "bias_tab": _bias_table(f(rel_bias[0])),
        "ident": np.eye(128, dtype=np.float32),
    }
    maps = []
    for c in range(8):
        b, half = c // 2, c % 2
        xe = np.zeros((TT, D), np.float32)
        xe[HALO:] = x[b, half * T:(half + 1) * T]
        corr = np.zeros((128, 64), np.float32)
        for g in range(4):
            w = 2 << g
            t = np.arange(16)
            cnt = np.minimum(t + 1, w) if half == 0 else np.full(16, w)
            corr[:, g * 16:(g + 1) * 16] = (1.0 / cnt.astype(np.float32))[None, :]
        if half == 1:
            xe[:HALO] = x[b, T - HALO:T]
            hw = np.full((128, 128), 2.0, np.float32)
        else:
            hw = np.zeros((128, 128), np.float32)
        m = dict(shared)
        m.update({"x_ext": xe, "mem": np.ascontiguousarray(mem[b]), "halo_w": hw, "corr": corr})
        maps.append(m)
    return maps


_NC_CACHE = {}


def kernel(**inputs):
    if "nc" not in _NC_CACHE:
        _NC_CACHE["nc"] = build_program()
    nc = _NC_CACHE["nc"]
    in_maps = make_in_maps(**inputs)
    res = run_bass_kernel_spmd(nc, in_maps, core_ids=list(range(8)))
    outs = [np.asarray(r["out"], dtype=np.float32) for r in res.results]
    full = np.zeros((4, 2048, D), np.float32)
    for c in range(8):
        b, half = c // 2, c % 2
        full[b, half * T:(half + 1) * T] = outs[c]
    return full
```

```python
import numpy as np
import concourse.bass as bass
import concourse.mybir as mybir
from concourse.bass_utils import run_bass_kernel_spmd

F32 = mybir.dt.float32
BF16 = mybir.dt.bfloat16
AF = mybir.ActivationFunctionType
ALU = mybir.AluOpType

D = 4096
T = 1024
HALO = 512
TT = T + HALO
NMEM = 256
KC = D // 128
EPS = 1e-6
NW = 3
NEG = -1e30


class Buf:
    __slots__ = ("w", "r")

    def __init__(self):
        self.w = None
        self.r = {}


class Stream:
    def __init__(self, name, is_dma_only=False):
        self.name = name
        self.ops = []
        self.cnt = 0
        self.waited = {}
        self.dma_i = 0
        self.dma_out = {}


class Prog:
    NDS = 8

    def __init__(self):
        self.streams = {n: Stream(n) for n in ("pe", "act", "dve", "pool", "sp")}
        self.sems = {}

    def sem_keys(self):
        keys = []
        for n in self.streams:
            keys.append("c_" + n)
        for n in ("pool", "sp", "act"):
            for i in range(self.NDS):
                keys.append("d_%s_%d" % (n, i))
        return keys

    def _waits(self, s, deps):
        waits = []
        for d in deps:
            if d is None:
                continue
            k, v = d
            if s.waited.get(k, 0) >= v:
                continue
            s.waited[k] = v
            waits.append((k, v))
        return waits

    def op(self, eng, fn, reads=(), writes=(), extra=()):
        s = self.streams[eng]
        own = "c_" + eng
        deps = list(extra)
        for b in reads:
            if b.w is not None:
                deps.append(b.w)
        for b in writes:
            if b.w is not None:
                deps.append(b.w)
            deps.extend(b.r.items())
        waits = self._waits(s, deps)
        s.cnt += 1
        tok = (own, s.cnt)
        s.ops.append((waits, fn, tok, 1))
        self._track(tok, reads, writes)
        return tok

    def dma(self, eng, fn, reads=(), writes=(), extra=()):
        s = self.streams[eng]
        nds = 3 if eng == "pool" else self.NDS
        slot = s.dma_i % nds
        rnd = s.dma_i // nds
        s.dma_i += 1
        key = "d_%s_%d" % (eng, slot)
        deps = list(extra)
        if rnd > 0:
            deps.append((key, 16 * rnd))
        for b in reads:
            if b.w is not None:
                deps.append(b.w)
        for b in writes:
            if b.w is not None:
                deps.append(b.w)
            deps.extend(b.r.items())
        waits = self._waits(s, deps)
        tok = (key, 16 * (rnd + 1))
        s.ops.append((waits, fn, tok, 16))
        s.dma_out[key] = tok[1]
        self._track(tok, reads, writes)
        return tok

    def _track(self, tok, reads, writes):
        for b in reads:
            if b.r.get(tok[0], 0) < tok[1]:
                b.r[tok[0]] = tok[1]
        for b in writes:
            b.w = tok
            b.r = {}

    def barrier(self):
        toks = []
        for n, s in self.streams.items():
            if s.cnt > 0:
                toks.append(("c_" + n, s.cnt))
            toks.extend(s.dma_out.items())
        for n, s in self.streams.items():
            waits = self._waits(s, toks)
            if waits:
                s.ops.append((waits, None, None, 0))

    def emit(self, eng, h):
        s = self.streams[eng]
        for waits, fn, tok, inc in s.ops:
            for k, v in waits:
                h.wait_ge(self.sems[k], v)
            if fn is None:
                continue
            ins = fn(h)
            ins.then_inc(self.sems[tok[0]], inc)


class Arena:
    def __init__(self, ap_bf16, nbytes):
        self.ap = ap_bf16
        self.nbytes = nbytes
        self.off = 0

    def alloc_at(self, off, shape, dtype):
        save = self.off
        self.off = off
        v = self.alloc(shape, dtype)
        self.off = save
        return v

    def alloc(self, shape, dtype):
        n = 1
        for d in shape:
            n *= d
        esz = 4 if dtype == F32 else 2
        nb = (n * esz + 63) // 64 * 64
        assert self.off + nb <= self.nbytes, ("SBUF arena overflow", self.off, nb, self.nbytes)
        v = self.ap[:, self.off // 2:(self.off + n * esz) // 2]
        self.off += nb
        if dtype == F32:
            v = v.bitcast(F32)
        if len(shape) == 2:
            v = v.rearrange("p (a b) -> p a b", a=shape[0])
        elif len(shape) == 3:
            v = v.rearrange("p (a b c) -> p a b c", a=shape[0], b=shape[1])
        return v


def build_program(debug=False):
    nc = bass.Bass("TRN2", target_bir_lowering=False)

    def din(name, shape, dt=F32):
        return nc.dram_tensor(name, list(shape), dt, kind="ExternalInput").ap()

    x_ext = din("x_ext", [TT, D])
    mem = din("mem", [NMEM, D])
    w_in = din("w_in", [D, 12288])
    w_pool = din("w_pool", [2048, 512])
    w_out = din("w_out", [D, D])
    w_xq = din("w_xq", [D, 1024])
    w_xk = din("w_xk", [D, 1024])
    w_xv = din("w_xv", [D, 1024])
    w_xo = din("w_xo", [1024, D])
    g_mix_pre = din("g_mix_pre", [1, D])
    g_mix_post = din("g_mix_post", [1, D])
    g_xa_pre = din("g_xa_pre", [1, D])
    g_mem = din("g_mem", [1, D])
    g_xa_post = din("g_xa_post", [1, D])
    pscale_in = din("pool_scale_col", [128, 16])
    bias_tab = din("bias_tab", [16, 128, 640])
    halo_in = din("halo_w", [128, 128])
    corr_in = din("corr", [128, 64])
    ident_in = din("ident", [128, 128])
    out = nc.dram_tensor("out", [T, D], F32, kind="ExternalOutput").ap()
    skind = "ExternalOutput" if debug else "Internal"
    ysT = nc.dram_tensor("ysT", [128, KC, T], BF16, kind=skind).ap()
    zs = nc.dram_tensor("zs", [T, D], F32, kind=skind).ap()
    x1s = nc.dram_tensor("x1s", [T, D], F32, kind=skind).ap()

    P = Prog()
    ARENA_BYTES = 207 * 1024

    with (
        nc.sbuf_tensor("arena", [128, ARENA_BYTES // 2], BF16) as arena_t,
        nc.psum_tensor("ps", [128, 4096], F32) as ps,
    ):
        A = Arena(arena_t[:, :], ARENA_BYTES)
        bank = [ps[:, b * 512:(b + 1) * 512] for b in range(8)]
        bankB = [Buf() for _ in range(8)]

        ident_f = A.alloc([128], F32)
        ident_b = A.alloc([128], BF16)
        ones_b = A.alloc([128], BF16)
        halo_f = A.alloc([128], F32)
        halo_b = A.alloc([128], BF16)
        pscale = A.alloc([16], F32)
        corr = A.alloc([64], F32)
        stat = A.alloc([64], F32)
        eps_col = A.alloc([2], F32)
        zstat = A.alloc([64], F32)
        zsum = A.alloc([8], F32)
        rstdz = A.alloc([8], F32)
        TOP = ARENA_BYTES - 8192
        K2T = A.alloc_at(TOP, [8, NMEM], BF16)
        K2TB = Buf()
        V2 = A.alloc_at(TOP + 4096, [2, 1024], BF16)
        V2B = Buf()
        wslot = [A.alloc([4096], BF16) for _ in range(NW)]
        wslotB = [Buf() for _ in range(NW)]
        constB = Buf()
        statBs = {}

        def sB(c):
            if c not in statBs:
                statBs[c] = Buf()
            return statBs[c]
        MARK0 = A.off

        units = []

        def wview(w, col0, ncol, kc0=0, kcn=None):
            v = w.rearrange("(kc p) n -> p kc n", p=128)
            kcn = v.shape[1] if kcn is None else kcn
            return v[:, kc0:kc0 + kcn, col0:col0 + ncol]

        for h in range(16):
            for base in (0, 2048, 4096, 6144):
                units.append((wview(w_in, base + h * 128, 128), 32, 128))
        for g in range(4):
            for c in range(4 * g, 4 * g + 4):
                units.append((wview(w_in, 8192 + c * 128, 128), 32, 128))
            for c in range(4 * g, 4 * g + 4):
                units.append((wview(w_in, 10240 + c * 128, 128), 32, 128))
        for u in range(8):
            units.append((wview(w_xk, u * 128, 128), 32, 128))
        for u in range(8):
            units.append((wview(w_xv, u * 128, 128), 32, 128))
        for u in range(8):
            units.append((wview(w_xq, u * 128, 128), 32, 128))
        wstate = {"loaded": 0, "next": 0}

        def w_load_upto(i):
            while wstate["loaded"] <= i and wstate["loaded"] < len(units):
                j = wstate["loaded"]
                src, kcn, ncol = units[j]
                dst = wslot[j % NW][:, 0:kcn * ncol].rearrange("p (k n) -> p k n", k=kcn)
                P.dma("pool", lambda h, d=dst, s=src: h.dma_start(out=d, in_=s),
                      writes=[wslotB[j % NW]])
                wstate["loaded"] += 1

        def w_get():
            i = wstate["next"]
            wstate["next"] += 1
            w_load_upto(i + NW - 1)
            src, kcn, ncol = units[i]
            v = wslot[i % NW][:, 0:kcn * ncol].rearrange("p (k n) -> p k n", k=kcn)
            return v, wslotB[i % NW]

        def issue_consts():
            cB = [Buf() for _ in range(4)]
            P.dma("sp", lambda h: h.dma_start(out=ident_f, in_=ident_in), writes=[cB[0]])
            P.dma("sp", lambda h: h.dma_start(out=halo_f, in_=halo_in), writes=[cB[1]])
            P.dma("sp", lambda h: h.dma_start(out=pscale, in_=pscale_in), writes=[cB[2]])
            P.dma("sp", lambda h: h.dma_start(out=corr, in_=corr_in), writes=[cB[3]])
            P.op("dve", lambda h: h.tensor_copy(out=ident_b, in_=ident_f), reads=[cB[0]], writes=[constB])
            P.op("dve", lambda h: h.tensor_copy(out=halo_b, in_=halo_f), reads=[cB[1]], writes=[constB])
            P.op("dve", lambda h: h.memset(ones_b, 2.0), writes=[constB])
            P.op("dve", lambda h: h.memset(eps_col, EPS), writes=[constB])
            P.op("dve", lambda h: h.tensor_scalar(out=pscale, in0=pscale, scalar1=0.5, scalar2=None,
                                                  op0=ALU.mult), reads=[cB[2], cB[3]], writes=[constB])
        w_load_upto(NW - 1)

        proj_rr = {"i": 0}

        def next_proj_bank():
            b = proj_rr["i"] % 2
            proj_rr["i"] += 1
            return b

        def norm_A(src, srcB, gbc, gbcB, hn, hnB, junk, junkB, scol):
            for c in range(2):
                P.op("act", lambda h, c=c: h.activation(out=junk, in_=src[:, c * 2048:(c + 1) * 2048],
                                                        func=AF.Square, accum_out=stat[:, scol + c:scol + c + 1]),
                     reads=[srcB], writes=[junkB, sB(scol + c)])
            P.op("dve", lambda h: h.tensor_tensor(out=stat[:, scol + 2:scol + 3], in0=stat[:, scol:scol + 1],
                                                  in1=stat[:, scol + 1:scol + 2], op=ALU.add),
                 reads=[sB(scol), sB(scol + 1)], writes=[sB(scol + 2)])
            P.op("act", lambda h: h.activation(out=stat[:, scol + 3:scol + 4], in_=stat[:, scol + 2:scol + 3],
                                               func=AF.Sqrt, scale=1.0 / D, bias=eps_col[:, 0:1]),
                 reads=[sB(scol + 2), constB], writes=[sB(scol + 3)])
            P.op("dve", lambda h: h.reciprocal(out=stat[:, scol + 4:scol + 5], in_=stat[:, scol + 3:scol + 4]),
                 reads=[sB(scol + 3)], writes=[sB(scol + 4)])
            P.op("dve", lambda h: h.scalar_tensor_tensor(out=hn, in0=src, scalar=stat[:, scol + 4:scol + 5],
                                                         in1=gbc, op0=ALU.mult, op1=ALU.mult),
                 reads=[srcB, sB(scol + 4), gbcB], writes=[hnB])

        def norm_B(hn, hnB, dst_fn, dstBs):
            for q in range(4):
                b = 4 + q
                pvb = bank[b].bitcast(BF16)

                def tr(h, q=q, pvb=pvb):
                    ins = None
                    for j in range(8):
                        kc = q * 8 + j
                        ins = h.transpose(out=pvb[:, j * 128:(j + 1) * 128],
                                          in_=hn[:, kc * 128:(kc + 1) * 128], identity=ident_b)
                    return ins
                P.op("pe", tr, reads=[hnB, constB], writes=[bankB[b]])
                dst = dst_fn(q)
                srcv = pvb.rearrange("p (k t) -> p k t", k=8)
                if q % 2 == 0:
                    P.op("act", lambda h, d=dst, s=srcv: h.activation(out=d, in_=s, func=AF.Copy),
                         reads=[bankB[b]], writes=[dstBs[0]])
                else:
                    P.op("dve", lambda h, d=dst, s=srcv: h.tensor_copy(out=d, in_=s),
                         reads=[bankB[b]], writes=[dstBs[1]])

        def norm_to_featmajor(src, srcB, gbc, gbcB, hn, hnB, junk, junkB, scol, dst_fn, dstBs):
            norm_A(src, srcB, gbc, gbcB, hn, hnB, junk, junkB, scol)
            norm_B(hn, hnB, dst_fn, dstBs)

        hT = A.alloc([KC, TT], BF16)
        hTB = [[Buf(), Buf()] for _ in range(TT // 128)]
        MARK_H = A.off
        xt0 = [A.alloc([D], F32) for _ in range(3)]
        xt0B = [Buf(), Buf(), Buf()]
        hn0 = [A.alloc([D], BF16) for _ in range(2)]
        hn0B = [Buf(), Buf()]
        junk0 = A.alloc([2048], BF16)
        junk0B = Buf()
        gbc0 = A.alloc([D], F32)
        gbc0B = Buf()
        NT0 = TT // 128

        def p0_load(i):
            P.dma("sp", lambda h: h.dma_start(out=xt0[i % 3], in_=x_ext[i * 128:(i + 1) * 128, :]),
                  writes=[xt0B[i % 3]])

        def p0_norm(i):
            norm_A(xt0[i % 3], xt0B[i % 3], gbc0, gbc0B, hn0[i % 2], hn0B[i % 2], junk0, junk0B, 5 * (i % 4))
        issue_consts()
        p0_load(0)
        P.dma("sp", lambda h: h.dma_start(out=gbc0, in_=g_mix_pre.partition_broadcast(128)[:, 0, :]),
              writes=[gbc0B])
        p0_load(1)
        p0_load(2)
        p0_norm(0)
        for i in range(NT0):
            if i + 1 < NT0:
                if i + 3 < NT0:
                    p0_load(i + 3)
                p0_norm(i + 1)
            norm_B(hn0[i % 2], hn0B[i % 2],
                   lambda q, i=i: hT[:, q * 8:q * 8 + 8, i * 128:(i + 1) * 128], hTB[i])
        P.barrier()
        A.off = MARK_H

        qT = [A.alloc([T], BF16) for _ in range(2)]
        kT = [A.alloc([TT], BF16) for _ in range(2)]
        vT = A.alloc([TT], BF16)
        V = [A.alloc([12, 128], BF16) for _ in range(2)]
        sg = [A.alloc([T], F32) for _ in range(2)]
        th = A.alloc([512], F32)
        btab = [A.alloc([640], F32) for _ in range(2)]
        sc = [A.alloc([640], F32) for _ in range(2)]
        pT = [A.alloc([640], BF16) for _ in range(2)]
        rc = [A.alloc([128], F32) for _ in range(2)]
        tt_ = [A.alloc([128], F32) for _ in range(2)]
        yst = [A.alloc([T], BF16) for _ in range(2)]
        qTB = [Buf(), Buf()]
        kTB = [Buf(), Buf()]
        vTB = Buf()
        VB = [Buf(), Buf()]
        sgB = [Buf(), Buf()]
        thB = Buf()
        btabB = [Buf(), Buf()]
        scB = [Buf(), Buf()]
        pTB = [Buf(), Buf()]
        rcB = [Buf(), Buf()]
        ttB = [Buf(), Buf()]
        ystB = [Buf(), Buf()]
        ysTB = [Buf() for _ in range(32)]

        def proj(wv, wB, kcn, rhs_fn, rhsB_fn, blocks, evac_fn):
            for bi, (t0, n) in enumerate(blocks):
                b = next_proj_bank()

                def mm(h, t0=t0, n=n, b=b):
                    ins = None
                    for kc in range(kcn):
                        ins = h.matmul(bank[b][:, 0:n], lhsT=wv[:, kc, :], rhs=rhs_fn(kc, t0, n),
                                       start=(kc == 0), stop=(kc == kcn - 1))
                    return ins
                P.op("pe", mm, reads=[wB] + list(rhsB_fn(t0, n)), writes=[bankB[b]])
                evac_fn(bi, t0, n, bank[b][:, 0:n], bankB[b])

        def hT_rhs(kc, t0, n):
            return hT[:, kc, t0:t0 + n]

        def hT_bufs(t0, n):
            r = []
            for i in range(t0 // 128, (t0 + n + 127) // 128):
                r.extend(hTB[i])
            return r

        OWN = [(HALO, 512), (HALO + 512, 512)]
        ALL = [(0, 512), (512, 512), (1024, 512)]
        SCALE_A = 128 ** -0.5

        def attn_chunks(hd):
            hb = hd % 2

            def qk(qb):
                sb = qb % 2
                bA, bB2 = (3, 4) if sb == 0 else (5, 6)

                def f(h):
                    ins = None
                    for j in range(5):
                        o = bank[bA][:, j * 128:(j + 1) * 128] if j < 4 else bank[bB2][:, 0:128]
                        ins = h.matmul(o, lhsT=kT[hb][:, (qb + j) * 128:(qb + j + 1) * 128],
                                       rhs=qT[hb][:, qb * 128:(qb + 1) * 128], start=True, stop=True)
                    return ins
                P.op("pe", f, reads=[kTB[hb], qTB[hb]], writes=[bankB[bA], bankB[bB2]])
                sview = ps[:, bA * 512:bA * 512 + 640]
                P.op("dve", lambda h: h.tensor_tensor(out=sc[sb], in0=sview, in1=btab[hb], op=ALU.add),
                     reads=[bankB[bA], bankB[bB2], btabB[hb]], writes=[scB[sb]])
                P.op("act", lambda h: h.activation(out=pT[sb], in_=sc[sb], func=AF.Exp),
                     reads=[scB[sb]], writes=[pTB[sb]])

            def pvden(qb):
                sb = qb % 2
                ob = 7 if qb % 2 == 0 else 2

                def f(h):
                    ins = None
                    for j in range(5):
                        ins = h.matmul(bank[ob][:, 0:128], lhsT=V[hb][:, qb + j, :],
                                       rhs=pT[sb][:, j * 128:(j + 1) * 128], start=(j == 0), stop=(j == 4))
                    for j in range(5):
                        lw = halo_b if (qb + j) < 4 else ones_b
                        ins = h.matmul(bank[ob][:, 128:256], lhsT=lw,
                                       rhs=pT[sb][:, j * 128:(j + 1) * 128], start=(j == 0), stop=(j == 4))
                    return ins
                P.op("pe", f, reads=[VB[hb], pTB[sb], constB], writes=[bankB[ob]])
                P.op("dve", lambda h: h.reciprocal(out=rc[sb], in_=bank[ob][:, 128:256]),
                     reads=[bankB[ob]], writes=[rcB[sb]])
                P.op("dve", lambda h: h.tensor_tensor(out=tt_[sb], in0=bank[ob][:, 0:128], in1=rc[sb], op=ALU.mult),
                     reads=[bankB[ob], rcB[sb]], writes=[ttB[sb]])
                P.op("dve", lambda h: h.tensor_tensor(out=yst[hb][:, qb * 128:(qb + 1) * 128], in0=tt_[sb],
                                                      in1=sg[hb][:, qb * 128:(qb + 1) * 128], op=ALU.mult),
                     reads=[ttB[sb], sgB[hb]], writes=[ystB[hb]])

            def store():
                P.dma("sp", lambda h: h.dma_start(out=ysT[:, hd, :], in_=yst[hb]),
                      reads=[ystB[hb]], writes=[ysTB[hd]])

            return [
                lambda: (qk(0), qk(1)),
                lambda: (pvden(0), qk(2), pvden(1), qk(3)),
                lambda: (pvden(2), qk(4), pvden(3), qk(5)),
                lambda: (pvden(4), qk(6), pvden(5), qk(7)),
                lambda: (pvden(6), pvden(7), store()),
            ]

        chunks = {}
        for it in range(18):
            hd = it
            hb = hd % 2
            if 1 <= it <= 16:
                chunks[it - 1] = attn_chunks(it - 1)
            if hd >= 16:
                if it == 16:
                    chunks[14][4]()
                    for c in chunks[15]:
                        c()
                break
            P.dma("sp", lambda h, hd=hd: h.dma_start(out=btab[hd % 2], in_=bias_tab[hd]),
                  writes=[btabB[hd % 2]])
            wv, wB = w_get()

            def ev_q(bi, t0, n, pa, pB, hb=hb):
                P.op("act", lambda h: h.activation(out=qT[hb][:, t0 - HALO:t0 - HALO + n], in_=pa,
                                                   func=AF.Copy, scale=SCALE_A),
                     reads=[pB], writes=[qTB[hb]])
            proj(wv, wB, 32, hT_rhs, hT_bufs, OWN, ev_q)
            if it >= 2:
                chunks[it - 2][4]()
            if it >= 1:
                chunks[it - 1][0]()
            wv, wB = w_get()

            def ev_k(bi, t0, n, pa, pB, hb=hb):
                P.op("dve", lambda h: h.tensor_copy(out=kT[hb][:, t0:t0 + n], in_=pa),
                     reads=[pB], writes=[kTB[hb]])
            proj(wv, wB, 32, hT_rhs, hT_bufs, ALL, ev_k)
            if it >= 1:
                chunks[it - 1][1]()
            wv, wB = w_get()

            def ev_v(bi, t0, n, pa, pB):
                P.op("act", lambda h: h.activation(out=vT[:, t0:t0 + n], in_=pa, func=AF.Copy),
                     reads=[pB], writes=[vTB])
            proj(wv, wB, 32, hT_rhs, hT_bufs, ALL, ev_v)
            for half in range(2):
                pv = bank[2].bitcast(BF16)

                def trv(h, half=half, pv=pv):
                    ins = None
                    for j in range(6):
                        t = half * 6 + j
                        ins = h.transpose(out=pv[:, j * 128:(j + 1) * 128],
                                          in_=vT[:, t * 128:(t + 1) * 128], identity=ident_b)
                    return ins
                P.op("pe", trv, reads=[vTB, constB], writes=[bankB[2]])
                P.op("dve", lambda h, half=half, pv=pv, hb=hb: h.tensor_copy(
                    out=V[hb][:, half * 6:half * 6 + 6, :],
                    in_=pv[:, 0:768].rearrange("p (a b) -> p a b", a=6)),
                    reads=[bankB[2]], writes=[VB[hb]])
            if it >= 1:
                chunks[it - 1][2]()
            wv, wB = w_get()

            def ev_g(bi, t0, n, pa, pB, hb=hb):
                P.op("act", lambda h: h.activation(out=th[:, 0:n], in_=pa, func=AF.Tanh, scale=0.5),
                     reads=[pB], writes=[thB])
                P.op("dve", lambda h: h.scalar_tensor_tensor(
                    out=sg[hb][:, t0 - HALO:t0 - HALO + n], in0=th[:, 0:n], scalar=1.0, in1=pa,
                    op0=ALU.add, op1=ALU.mult), reads=[pB, thB], writes=[sgB[hb]])
            proj(wv, wB, 32, hT_rhs, hT_bufs, OWN, ev_g)
            if it >= 1:
                chunks[it - 1][3]()

        P.barrier()
        A.off = MARK_H

        UW = T + 16
        u_ext = [A.alloc([UW], F32) for _ in range(2)]
        s_a = A.alloc([UW], F32)
        s_b = A.alloc([UW], F32)
        t16 = A.alloc([16], F32)
        dT = A.alloc([4, T], BF16)
        sgb = [A.alloc([T], F32) for _ in range(2)]
        thb = A.alloc([512], F32)
        ystb = [A.alloc([T], BF16) for _ in range(2)]
        wp = [A.alloc([4, 512], BF16) for _ in range(2)]
        wpB = [Buf(), Buf()]
        u_extB = [Buf(), Buf()]
        s_aB = Buf()
        s_bB = Buf()
        t16B = Buf()
        dTB = [Buf() for _ in range(4)]
        sgbB = [Buf(), Buf()]
        thbB = Buf()
        ystbB = [Buf(), Buf()]
        UBLK = [(HALO - 16, 512), (HALO - 16 + 512, 512), (HALO - 16 + 1024, 16)]

        for g in range(4):
            wlen = 2 << g
            P.dma("pool", lambda h, g=g: h.dma_start(out=wp[g % 2], in_=wview(w_pool, 0, 512, kc0=4 * g, kcn=4)),
                  writes=[wpB[g % 2]])
            for ci in range(4):
                c = 4 * g + ci
                ub = c % 2
                wv, wB = w_get()

                def ev_u(bi, t0, n, pa, pB, ub=ub):
                    o = t0 - (HALO - 16)
                    P.op("act", lambda h: h.activation(out=u_ext[ub][:, o:o + n], in_=pa, func=AF.Copy),
                         reads=[pB], writes=[u_extB[ub]])
                proj(wv, wB, 32, hT_rhs, hT_bufs, UBLK, ev_u)
                u = u_ext[ub]
                cur, curB, lo = u, u_extB[ub], 0
                tmp = [(s_a, s_aB), (s_b, s_bB)]
                step = 1
                k = 0
                while step < wlen:
                    dst, dstB = tmp[k % 2]
                    nlo = lo + step
                    P.op("dve", lambda h, dst=dst, cur=cur, nlo=nlo, step=step: h.tensor_tensor(
                        out=dst[:, nlo:UW], in0=cur[:, nlo:UW], in1=cur[:, nlo - step:UW - step], op=ALU.add),
                        reads=[curB], writes=[dstB])
                    cur, curB, lo = dst, dstB, nlo
                    step *= 2
                    k += 1
                P.op("dve", lambda h, cur=cur, u=u, ci=ci, wlen=wlen: h.scalar_tensor_tensor(
                    out=dT[:, ci, 16:T], in0=cur[:, 32:UW], scalar=1.0 / wlen, in1=u[:, 32:UW],
                    op0=ALU.mult, op1=ALU.subtract), reads=[curB, u_extB[ub]], writes=[dTB[ci]])
                P.op("dve", lambda h, cur=cur, g=g: h.tensor_tensor(
                    out=t16, in0=cur[:, 16:32], in1=corr[:, g * 16:(g + 1) * 16], op=ALU.mult),
                    reads=[curB, constB], writes=[t16B])
                P.op("dve", lambda h, u=u, ci=ci: h.tensor_tensor(
                    out=dT[:, ci, 0:16], in0=t16, in1=u[:, 16:32], op=ALU.subtract),
                    reads=[t16B, u_extB[ub]], writes=[dTB[ci]])
            wpv, wpB_ = wp[g % 2], wpB[g % 2]
            for oc in range(4):
                c = 4 * g + oc
                sb_ = c % 2
                wv, wB = w_get()

                def ev_gb(bi, t0, n, pa, pB, sb_=sb_):
                    P.op("act", lambda h: h.activation(out=thb[:, 0:n], in_=pa, func=AF.Tanh, scale=0.5),
                         reads=[pB], writes=[thbB])
                    P.op("dve", lambda h: h.scalar_tensor_tensor(
                        out=sgb[sb_][:, t0 - HALO:t0 - HALO + n], in0=thb[:, 0:n], scalar=1.0, in1=pa,
                        op0=ALU.add, op1=ALU.mult), reads=[pB, thbB], writes=[sgbB[sb_]])
                proj(wv, wB, 32, hT_rhs, hT_bufs, OWN, ev_gb)
                for blk in range(2):
                    b = next_proj_bank()

                    def mmp(h, blk=blk, b=b, oc=oc, wpv=wpv):
                        ins = None
                        for ic in range(4):
                            ins = h.matmul(bank[b], lhsT=wpv[:, ic, oc * 128:(oc + 1) * 128],
                                           rhs=dT[:, ic, blk * 512:(blk + 1) * 512],
                                           start=(ic == 0), stop=(ic == 3))
                        return ins
                    P.op("pe", mmp, reads=[wpB_] + dTB, writes=[bankB[b]])
                    P.op("dve", lambda h, blk=blk, b=b, c=c, sb_=sb_: h.scalar_tensor_tensor(
                        out=ystb[sb_][:, blk * 512:(blk + 1) * 512], in0=bank[b], scalar=pscale[:, c:c + 1],
                        in1=sgb[sb_][:, blk * 512:(blk + 1) * 512], op0=ALU.mult, op1=ALU.mult),
                        reads=[bankB[b], sgbB[sb_], constB], writes=[ystbB[sb_]])
                P.dma("sp", lambda h, c=c, sb_=sb_: h.dma_start(
                    out=ysT[:, 16 + c, :], in_=ystb[sb_]),
                    reads=[ystbB[sb_]], writes=[ysTB[16 + c]])

        P.barrier()
        A.off = MARK0

        mT = A.alloc([KC, NMEM], BF16)
        mTB = [[Buf(), Buf()], [Buf(), Buf()]]
        yT = A.alloc([KC, T], BF16)
        yTB = [Buf() for _ in range(4)]
        wbig = [None, None]
        wbig[0] = A.alloc([KC, 512], BF16)
        off_w1 = A.off
        wbig[1] = A.alloc([KC, 512], BF16)
        wbigB = [Buf(), Buf()]
        zst_b = [A.alloc([512], F32) for _ in range(2)]
        zst_bB = [Buf(), Buf()]
        junkz = A.alloc([512], BF16)
        junkzB = Buf()
        off_g = A.off
        gbcp = A.alloc([D], F32)
        gbcpB = Buf()
        v2T = A.alloc([NMEM], BF16)
        v2TB = Buf()
        off_end = A.off
        assert A.off <= TOP
        A.off = off_w1
        xtm = A.alloc([D], F32)
        gbcm = A.alloc([D], F32)
        A.off = off_g
        hnm = A.alloc([D], BF16)
        junkm = A.alloc([2048], BF16)
        A.off = off_end
        xtmB = Buf()
        gbcmB = Buf()
        hnmB = Buf()
        junkmB = Buf()

        zsB = [Buf() for _ in range(8)]
        zstatB = Buf()
        ysv = ysT
        for q in range(4):
            P.dma("sp", lambda h, q=q: h.dma_start(out=yT[:, q * 8:(q + 1) * 8, :], in_=ysv[:, q * 8:(q + 1) * 8, :]),
                  reads=ysTB[q * 8:(q + 1) * 8], writes=[yTB[q]])
        P.dma("sp", lambda h: h.dma_start(out=gbcm, in_=g_mem.partition_broadcast(128)[:, 0, :]), writes=[gbcmB])
        for i in range(2):
            P.dma("sp", lambda h, i=i: h.dma_start(out=xtm, in_=mem[i * 128:(i + 1) * 128, :]),
                  writes=[xtmB])
            norm_to_featmajor(xtm, xtmB, gbcm, gbcmB, hnm, hnmB, junkm, junkmB, 5 * i,
                              lambda q, i=i: mT[:, q * 8:q * 8 + 8, i * 128:(i + 1) * 128], mTB[i])
        P.dma("sp", lambda h: h.dma_start(out=gbcp, in_=g_mix_post.partition_broadcast(128)[:, 0, :]),
              writes=[gbcpB, hnmB, junkmB])
        MBLK = [(0, NMEM)]
        mT_all = mTB[0] + mTB[1]

        def mT_rhs(kc, t0, n):
            return mT[:, kc, t0:t0 + n]

        def kv_unit(j):
            wv, wB = w_get()
            if j < 8:
                def ev_k2(bi, t0, n, pa, pB, u=j):
                    P.op("act", lambda h: h.activation(out=K2T[:, u, :], in_=pa, func=AF.Copy),
                         reads=[pB], writes=[K2TB])
                proj(wv, wB, 32, mT_rhs, lambda t0, n: mT_all, MBLK, ev_k2)
            else:
                u = j - 8

                def ev_v2(bi, t0, n, pa, pB):
                    P.op("act", lambda h: h.activation(out=v2T, in_=pa, func=AF.Copy),
                         reads=[pB], writes=[v2TB])
                proj(wv, wB, 32, mT_rhs, lambda t0, n: mT_all, MBLK, ev_v2)
                pv = bank[2].bitcast(BF16)

                def trv2(h, pv=pv):
                    ins = None
                    for jj in range(2):
                        ins = h.transpose(out=pv[:, jj * 128:(jj + 1) * 128], in_=v2T[:, jj * 128:(jj + 1) * 128],
                                          identity=ident_b)
                    return ins
                P.op("pe", trv2, reads=[v2TB, constB], writes=[bankB[2]])
                P.op("dve", lambda h, u=u, pv=pv: h.tensor_copy(
                    out=V2[:, :, u * 128:(u + 1) * 128], in_=pv[:, 0:256].rearrange("p (a b) -> p a b", a=2)),
                    reads=[bankB[2]], writes=[V2B])

        def stats_to_rstd(ncol, src_cols, dst_col, stB, rsB):
            P.op("dve", lambda h: h.tensor_reduce(out=zsum[:, dst_col:dst_col + 1],
                                                  in_=zstat[:, src_cols:src_cols + ncol],
                                                  axis=mybir.AxisListType.X, op=ALU.add),
                 reads=[stB], writes=[rsB])
            P.op("act", lambda h: h.activation(out=zsum[:, dst_col:dst_col + 1], in_=zsum[:, dst_col:dst_col + 1],
                                               func=AF.Sqrt, scale=1.0 / D, bias=eps_col[:, 0:1]),
                 reads=[rsB, constB], writes=[rsB])
            P.op("dve", lambda h: h.reciprocal(out=rstdz[:, dst_col:dst_col + 1], in_=zsum[:, dst_col:dst_col + 1]),
                 reads=[rsB], writes=[rsB])

        zstB = [Buf() for _ in range(8)]
        rsB1 = [Buf() for _ in range(8)]

        def load_big(cb):
            extra_w = [xtmB, gbcmB] if cb == 1 else []
            P.dma("pool", lambda h, cb=cb: h.dma_start(out=wbig[cb % 2], in_=wview(w_out, cb * 512, 512)),
                  writes=[wbigB[cb % 2]] + extra_w)
        load_big(0)
        load_big(1)

        step = 0
        for cb in range(8):
            wv, wB = wbig[cb % 2], wbigB[cb % 2]
            for tile in range(8):
                b = next_proj_bank()

                def mm(h, tile=tile, b=b, wv=wv):
                    ins = None
                    for kc in range(32):
                        ins = h.matmul(bank[b], lhsT=yT[:, kc, tile * 128:(tile + 1) * 128], rhs=wv[:, kc, :],
                                       start=(kc == 0), stop=(kc == 31))
                    return ins
                P.op("pe", mm, reads=[wB] + yTB, writes=[bankB[b]])
                col = tile * 8 + cb
                ta = P.op("act", lambda h, b=b, col=col: h.activation(
                    out=junkz, in_=bank[b], func=AF.Square, accum_out=zstat[:, col:col + 1]),
                    reads=[bankB[b]], writes=[junkzB, zstB[tile]])
                zb = step % 2
                P.op("dve", lambda h, b=b, zb=zb, cb=cb: h.tensor_tensor(
                    out=zst_b[zb], in0=bank[b], in1=gbcp[:, cb * 512:(cb + 1) * 512], op=ALU.mult),
                    reads=[bankB[b], gbcpB], writes=[zst_bB[zb]], extra=[ta])
                P.dma("sp", lambda h, tile=tile, cb=cb, zb=zb: h.dma_start(
                    out=zs[tile * 128:(tile + 1) * 128, cb * 512:(cb + 1) * 512], in_=zst_b[zb]),
                    reads=[zst_bB[zb]], writes=[zsB[tile]])
                step += 1
                if step % 4 == 0:
                    kv_unit(step // 4 - 1)
            if cb + 2 < 8:
                load_big(cb + 2)
        P.op("dve", lambda h: h.tensor_reduce(out=zsum, in_=zstat.rearrange("p (a b) -> p a b", a=8),
                                              axis=mybir.AxisListType.X, op=ALU.add),
             reads=zstB, writes=rsB1)
        P.op("act", lambda h: h.activation(out=zsum, in_=zsum, func=AF.Sqrt, scale=1.0 / D, bias=eps_col[:, 0:1]),
             reads=rsB1 + [constB], writes=rsB1)
        P.op("dve", lambda h: h.reciprocal(out=rstdz, in_=zsum), reads=rsB1, writes=rsB1)
        P.barrier()
        A.off = MARK0

        h2T = A.alloc([KC, T], BF16)
        h2TB = [[Buf(), Buf()] for _ in range(8)]
        MARK_C = A.off
        zin1 = [A.alloc([D], F32) for _ in range(2)]
        zin1B = [Buf(), Buf()]
        xt1 = [A.alloc([D], F32) for _ in range(2)]
        xt1B = [Buf(), Buf()]
        hn1 = [A.alloc([D], BF16) for _ in range(2)]
        hn1B = [Buf(), Buf()]
        junk1 = A.alloc([2048], BF16)
        junk1B = Buf()
        gbc2 = A.alloc([D], F32)
        gbc2B = Buf()
        assert A.off <= TOP
        x1sB = [Buf() for _ in range(8)]
        P.dma("sp", lambda h: h.dma_start(out=gbc2, in_=g_xa_pre.partition_broadcast(128)[:, 0, :]),
              writes=[gbc2B])

        def pb2_load(i):
            b2 = i % 2
            P.dma("sp", lambda h: h.dma_start(out=zin1[b2], in_=zs[i * 128:(i + 1) * 128, :]),
                  reads=[zsB[i]], writes=[zin1B[b2]])
            P.dma("sp", lambda h: h.dma_start(out=xt1[b2], in_=x_ext[HALO + i * 128:HALO + (i + 1) * 128, :]),
                  writes=[xt1B[b2]])

        def pb2_rest(i):
            b2 = i % 2
            P.op("dve", lambda h: h.scalar_tensor_tensor(out=xt1[b2], in0=zin1[b2], scalar=rstdz[:, i:i + 1],
                                                         in1=xt1[b2], op0=ALU.mult, op1=ALU.add),
                 reads=[zin1B[b2], xt1B[b2], rsB1[i]], writes=[xt1B[b2]])
            norm_A(xt1[b2], xt1B[b2], gbc2, gbc2B, hn1[b2], hn1B[b2], junk1, junk1B, 20 + 5 * (i % 4))

        def pb2_store(i):
            b2 = i % 2
            P.dma("sp", lambda h: h.dma_start(out=x1s[i * 128:(i + 1) * 128, :], in_=xt1[b2]),
                  reads=[xt1B[b2]], writes=[x1sB[i]])

        pb2_load(0)
        pb2_load(1)
        pb2_rest(0)
        pb2_store(0)
        for i in range(8):
            if i + 1 < 8:
                pb2_rest(i + 1)
            if i + 2 < 8:
                pb2_load(i + 2)
            if i + 1 < 8:
                pb2_store(i + 1)
            norm_B(hn1[i % 2], hn1B[i % 2],
                   lambda q, i=i: h2T[:, q * 8:q * 8 + 8, i * 128:(i + 1) * 128], h2TB[i])
        P.barrier()
        A.off = MARK_C

        o2T = A.alloc([8, T], BF16)
        o2TB = Buf()
        wxo = A.alloc([8, D], BF16)
        wxoB = [Buf() for _ in range(8)]
        MARK_D = A.off
        q2T = A.alloc([8, T], BF16)
        q2TB = Buf()
        p2T = [A.alloc([512], BF16) for _ in range(2)]
        p2TB = [Buf(), Buf()]
        rc2 = A.alloc([512], F32)
        rc2B = Buf()
        assert A.off <= TOP
        SCALE_X = 256 ** -0.5
        YBLK = [(0, 512), (512, 512)]

        def h2T_rhs(kc, t0, n):
            return h2T[:, kc, t0:t0 + n]

        def h2T_bufs(t0, n):
            r = []
            for i in range(t0 // 128, (t0 + n) // 128):
                r.extend(h2TB[i])
            return r

        for u in range(8):
            wv, wB = w_get()

            def ev_q2(bi, t0, n, pa, pB, u=u):
                P.op("act", lambda h: h.activation(out=q2T[:, u, t0:t0 + n], in_=pa, func=AF.Copy, scale=SCALE_X),
                     reads=[pB], writes=[q2TB])
            proj(wv, wB, 32, h2T_rhs, h2T_bufs, YBLK, ev_q2)
            P.dma("pool", lambda h, u=u: h.dma_start(out=wxo[:, :, u * 512:(u + 1) * 512],
                                                     in_=wview(w_xo, u * 512, 512)), writes=[wxoB[u]])

        for hd in range(4):
            for tb in range(2):
                t0 = tb * 512
                for mt in range(2):
                    b = 3 + mt

                    def s2(h, hd=hd, t0=t0, mt=mt, b=b):
                        ins = None
                        for dc in range(2):
                            ins = h.matmul(bank[b], lhsT=K2T[:, 2 * hd + dc, mt * 128:(mt + 1) * 128],
                                           rhs=q2T[:, 2 * hd + dc, t0:t0 + 512], start=(dc == 0), stop=(dc == 1))
                        return ins
                    P.op("pe", s2, reads=[K2TB, q2TB], writes=[bankB[b]])
                    P.op("act", lambda h, mt=mt, b=b: h.activation(out=p2T[mt], in_=bank[b], func=AF.Exp),
                         reads=[bankB[b]], writes=[p2TB[mt]])

                def dn(h):
                    ins = None
                    for mt in range(2):
                        ins = h.matmul(bank[7], lhsT=ones_b, rhs=p2T[mt], start=(mt == 0), stop=(mt == 1))
                    return ins
                P.op("pe", dn, reads=p2TB + [constB], writes=[bankB[7]])
                P.op("act", lambda h: h.activation(out=rc2, in_=bank[7], func=AF.Ln), reads=[bankB[7]], writes=[rc2B])
                P.op("act", lambda h: h.activation(out=rc2, in_=rc2, func=AF.Exp, scale=-1.0),
                     reads=[rc2B], writes=[rc2B])
                for dc in range(2):
                    b = 5 + dc

                    def o2(h, hd=hd, dc=dc, b=b):
                        ins = None
                        for mt in range(2):
                            ins = h.matmul(bank[b], lhsT=V2[:, mt, (2 * hd + dc) * 128:(2 * hd + dc + 1) * 128],
                                           rhs=p2T[mt], start=(mt == 0), stop=(mt == 1))
                        return ins
                    P.op("pe", o2, reads=p2TB + [V2B], writes=[bankB[b]])
                    P.op("dve", lambda h, hd=hd, dc=dc, b=b, t0=t0: h.scalar_tensor_tensor(
                        out=o2T[:, 2 * hd + dc, t0:t0 + 512], in0=bank[b], scalar=2.0, in1=rc2,
                        op0=ALU.mult, op1=ALU.mult), reads=[bankB[b], rc2B], writes=[o2TB])
        P.barrier()

        A.off = MARK0
        z2r = [A.alloc([D], F32) for _ in range(2)]
        z2rB = [Buf(), Buf()]
        xtf = [A.alloc([D], F32) for _ in range(2)]
        xtfB = [Buf(), Buf()]
        assert A.off <= MARK_C
        A.off = MARK_D
        gbcf = A.alloc([D], F32)
        gbcfB = Buf()
        junkc = A.alloc([512], BF16)
        junkcB = Buf()
        outB = [Buf() for _ in range(8)]
        P.dma("sp", lambda h: h.dma_start(out=gbcf, in_=g_xa_post.partition_broadcast(128)[:, 0, :]),
              writes=[gbcfB])
        zstB2 = [Buf(), Buf()]
        rsB2 = [Buf(), Buf()]

        def pc2_tail(tile):
            b2 = tile % 2
            stats_to_rstd(8, b2 * 8, b2, zstB2[b2], rsB2[b2])
            P.op("dve", lambda h: h.scalar_tensor_tensor(out=xtf[b2], in0=z2r[b2], scalar=rstdz[:, b2:b2 + 1],
                                                         in1=xtf[b2], op0=ALU.mult, op1=ALU.add),
                 reads=[z2rB[b2], xtfB[b2], rsB2[b2]], writes=[xtfB[b2]])
            P.dma("sp", lambda h: h.dma_start(out=out[tile * 128:(tile + 1) * 128, :], in_=xtf[b2]),
                  reads=[xtfB[b2]], writes=[outB[tile]])

        for tile in range(8):
            b2 = tile % 2
            P.dma("sp", lambda h, tile=tile, b2=b2: h.dma_start(out=xtf[b2], in_=x1s[tile * 128:(tile + 1) * 128, :]),
                  reads=[x1sB[tile]], writes=[xtfB[b2]])
            for cb in range(8):
                b = (tile * 8 + cb) % 6

                def mm2(h, tile=tile, cb=cb, b=b):
                    ins = None
                    for kc in range(8):
                        ins = h.matmul(bank[b], lhsT=o2T[:, kc, tile * 128:(tile + 1) * 128],
                                       rhs=wxo[:, kc, cb * 512:(cb + 1) * 512], start=(kc == 0), stop=(kc == 7))
                    return ins
                P.op("pe", mm2, reads=[o2TB, wxoB[cb]], writes=[bankB[b]])
                col = b2 * 8 + cb
                ta = P.op("act", lambda h, b=b, col=col: h.activation(
                    out=junkc, in_=bank[b], func=AF.Square, accum_out=zstat[:, col:col + 1]),
                    reads=[bankB[b]], writes=[junkcB, zstB2[b2]])
                P.op("dve", lambda h, b=b, b2=b2, cb=cb: h.tensor_tensor(
                    out=z2r[b2][:, cb * 512:(cb + 1) * 512], in0=bank[b], in1=gbcf[:, cb * 512:(cb + 1) * 512],
                    op=ALU.mult), reads=[bankB[b], gbcfB], writes=[z2rB[b2]], extra=[ta])
                if cb == 3 and tile >= 1:
                    pc2_tail(tile - 1)
        pc2_tail(7)
        P.barrier()

        keys = P.sem_keys()
        import contextlib
        with contextlib.ExitStack() as es:
            for k in keys:
                P.sems[k] = es.enter_context(nc.semaphore(k))
            with nc.Block() as block:
                @block.tensor
                def _(e):
                    P.emit("pe", e)

                @block.scalar
                def _(e):
                    P.emit("act", e)

                @block.vector
                def _(e):
                    P.emit("dve", e)

                @block.gpsimd
                def _(e):
                    P.emit("pool", e)

                @block.sync
                def _(e):
                    P.emit("sp", e)
    return nc


def _bias_table(rel_bias):
    kk = np.arange(128)[:, None, None]
    j = np.arange(5)[None, :, None]
    qq = np.arange(128)[None, None, :]
    koff = (j - 4) * 128 + kk
    dist = qq - koff
    idx = np.clip(dist, -63, 128) + 63
    cq = qq // 64
    ck = np.floor_divide(koff, 64)
    valid = (ck >= cq - 8) & (ck <= cq)
    tab = rel_bias[:, idx]
    tab = np.where(valid[None], tab, np.float32(NEG)).astype(np.float32)
    return np.ascontiguousarray(tab.reshape(16, 128, 640))


def make_in_maps(x, mem, g_mix_pre, w_in, rel_bias, w_pool, pool_scale, w_out, g_mix_post,
                 g_xa_pre, g_mem, w_xq, w_xk, w_xv, w_xo, g_xa_post):
    f = lambda a: np.ascontiguousarray(np.asarray(a, dtype=np.float32))
    x = f(x)
    mem = f(mem)
    shared = {
        "w_in": f(w_in[0]), "w_pool": f(np.asarray(w_pool[0]).reshape(2048, 512)), "w_out": f(w_out[0]),
        "w_xq": f(w_xq[0]), "w_xk": f(w_xk[0]), "w_xv": f(w_xv[0]), "w_xo": f(w_xo[0]),
        "g_mix_pre": f(g_mix_pre[0:1]), "g_mix_post": f(g_mix_post[0:1]), "g_xa_pre": f(g_xa_pre[0:1]),
        "g_mem": f(g_mem[0:1]), "g_xa_post": f(g_xa_post[0:1]),
        "pool_scale_col": f(np.asarray(pool_scale[0]).reshape(16, 128).T),
        "bias_tab": _bias_table(f(rel_bias[0])),
        "ident": np.eye(128, dtype=np.float32),
    }
    maps = []
    for c in range(8):
        b, half = c // 2, c % 2
        xe = np.zeros((TT, D), np.float32)
        xe[HALO:] = x[b, half * T:(half + 1) * T]
        corr = np.zeros((128, 64), np.float32)
        for g in range(4):
            w = 2 << g
            t = np.arange(16)
            cnt = np.minimum(t + 1, w) if half == 0 else np.full(16, w)
            corr[:, g * 16:(g + 1) * 16] = (1.0 / cnt.astype(np.float32))[None, :]
        if half == 1:
            xe[:HALO] = x[b, T - HALO:T]
            hw = np.full((128, 128), 2.0, np.float32)
        else:
            hw = np.zeros((128, 128), np.float32)
        m = dict(shared)
        m.update({"x_ext": xe, "mem": np.ascontiguousarray(mem[b]), "halo_w": hw, "corr": corr})
        maps.append(m)
    return maps


_NC_CACHE = {}


def kernel(**inputs):
    if "nc" not in _NC_CACHE:
        _NC_CACHE["nc"] = build_program()
    nc = _NC_CACHE["nc"]
    in_maps = make_in_maps(**inputs)
    res = run_bass_kernel_spmd(nc, in_maps, core_ids=list(range(8)))
    outs = [np.asarray(r["out"], dtype=np.float32) for r in res.results]
    full = np.zeros((4, 2048, D), np.float32)
    for c in range(8):
        b, half = c // 2, c % 2
        full[b, half * T:(half + 1) * T] = outs[c]
    return full
```
